# Optimizing a Trainium2 kernel written in Bass

```python
import math
import jax, jax.numpy as jnp
from jax import lax
import numpy as np

D_MODEL = 4096
BATCH = 4
SEQ = 4096
DEPTH = 2

GDN_HEADS = 16
GDN_DK = 128
GDN_DV = 128
GDN_CONV = 4
GDN_CHUNK = 64
NSA_HEADS = 16
NSA_KV_GROUPS = 4
NSA_DK = 128
NSA_DV = 128
CMP_BLOCK = 32
CMP_STRIDE = 16
SLC_BLOCK = 64
SLC_TOPK = 16
WINDOW = 512
NSA_Q_BLOCK = 64
D_FF = 4 * D_MODEL
N_ADA = 6
DN_ALPHA = (2.0 * DEPTH) ** 0.25
DN_BETA = (8.0 * DEPTH) ** -0.25
LN_EPS = 1e-5
RMS_EPS = 1e-6
NEG_INF = -1e30
FORCED_SCORE = 1e6

GDN_QK = GDN_HEADS * GDN_DK
GDN_VW = GDN_HEADS * GDN_DV
NSA_QW = NSA_HEADS * NSA_DK
NSA_KW = NSA_KV_GROUPS * NSA_DK
NSA_VW = NSA_KV_GROUPS * NSA_DV
NSA_OW = NSA_HEADS * NSA_DV

IN_SIZES = (GDN_QK, GDN_QK, GDN_VW, GDN_VW, GDN_HEADS, GDN_HEADS,
            NSA_QW, NSA_KW, NSA_VW, NSA_KW, NSA_VW, NSA_KW, NSA_VW, 3 * NSA_HEADS,
            D_MODEL, D_MODEL)
IN_IS_VALUE = (False, False, True, False, False, False,
               False, False, True, False, True, False, True, False,
               False, False)
N_IN = sum(IN_SIZES)

kernel_name = 'hybrid_gdn_nsa_deepnorm_adaln_block'


def _split_points():
    pts, acc = [], 0
    for n in IN_SIZES[:-1]:
        acc += n
        pts.append(acc)
    return pts


def layer_norm(x, g, b):
    xf = x.astype(jnp.float32)
    mu = jnp.mean(xf, axis=-1, keepdims=True)
    var = jnp.mean(jnp.square(xf - mu), axis=-1, keepdims=True)
    return ((xf - mu) * lax.rsqrt(var + LN_EPS) * g.astype(jnp.float32) + b.astype(jnp.float32)).astype(x.dtype)


def l2_normalize(t):
    return t * lax.rsqrt(jnp.sum(jnp.square(t), axis=-1, keepdims=True) + RMS_EPS)


def alibi_slopes(n_heads):
    return jnp.exp2(-8.0 * jnp.arange(1, n_heads + 1, dtype=jnp.float32) / n_heads)


def causal_depthwise_conv(t, w):
    ch = t.shape[-1]
    return lax.conv_general_dilated(t, w[:, None, :].astype(t.dtype), window_strides=(1,),
                                    padding=[(w.shape[0] - 1, 0)],
                                    dimension_numbers=('NWC', 'WIO', 'NWC'),
                                    feature_group_count=ch)


def gated_delta_rule(q, k, v, beta, log_a):
    B, S, H, DK = q.shape
    DV = v.shape[-1]
    C = GDN_CHUNK
    N = S // C

    def to_chunks(t):
        return jnp.swapaxes(t.reshape(B, N, C, H, *t.shape[3:]), 2, 3)

    q, k, v, beta, log_a = (to_chunks(t) for t in (q, k, v, beta, log_a))
    g = jnp.cumsum(log_a, axis=-1)
    ti = jnp.arange(C)
    incl = ti[:, None] >= ti[None, :]
    strict = ti[:, None] > ti[None, :]
    diff = g[..., :, None] - g[..., None, :]
    decay = jnp.where(incl, jnp.exp(jnp.where(incl, diff, 0.0)), 0.0)
    kk = jnp.einsum('bnhtd,bnhsd->bnhts', k, k)
    a_mat = jnp.where(strict, beta[..., :, None] * kk * decay, 0.0) + jnp.eye(C, dtype=q.dtype)
    w = lax.linalg.triangular_solve(a_mat, (beta * jnp.exp(g))[..., None] * k,
                                    left_side=True, lower=True, unit_diagonal=True)
    u = lax.linalg.triangular_solve(a_mat, beta[..., None] * v,
                                    left_side=True, lower=True, unit_diagonal=True)
    qk = jnp.einsum('bnhtd,bnhsd->bnhts', q, k) * decay
    g_last = g[..., -1]
    k_end = k * jnp.exp(g_last[..., None] - g)[..., None]
    q_g = q * jnp.exp(g)[..., None]

    def step(state, xs):
        w_n, u_n, q_n, qk_n, k_n, gl_n = xs
        u_corr = u_n - jnp.einsum('bhcd,bhde->bhce', w_n, state)
        o = jnp.einsum('bhcd,bhde->bhce', q_n, state) + jnp.einsum('bhts,bhse->bhte', qk_n, u_corr)
        state = jnp.exp(gl_n)[..., None, None] * state + jnp.einsum('bhcd,bhce->bhde', k_n, u_corr)
        return state, o

    xs = tuple(jnp.swapaxes(t, 0, 1) for t in (w, u, q_g, qk, k_end, g_last))
    state0 = jnp.zeros((B, H, DK, DV), q.dtype)
    _, o = lax.scan(step, state0, xs)
    return jnp.swapaxes(jnp.swapaxes(o, 0, 1), 2, 3).reshape(B, S, H, DV)


def gated_deltanet_branch(pq, pk, pv, pz, pb, pa, conv_w, a_log, dt_bias, norm_w):
    B, S, _ = pq.shape
    f32 = jnp.float32
    qkv = jax.nn.silu(causal_depthwise_conv(jnp.concatenate([pq, pk, pv], axis=-1), conv_w)).astype(f32)
    q, k, v = jnp.split(qkv, [GDN_QK, 2 * GDN_QK], axis=-1)
    q = l2_normalize(q.reshape(B, S, GDN_HEADS, GDN_DK)) * GDN_DK ** -0.5
    k = l2_normalize(k.reshape(B, S, GDN_HEADS, GDN_DK))
    v = v.reshape(B, S, GDN_HEADS, GDN_DV)
    beta = jax.nn.sigmoid(pb.astype(f32))
    log_a = -jnp.exp(a_log.astype(f32)) * jax.nn.softplus(pa.astype(f32) + dt_bias.astype(f32))
    o = gated_delta_rule(q, k, v, beta, log_a)
    o = o * lax.rsqrt(jnp.mean(jnp.square(o), axis=-1, keepdims=True) + RMS_EPS) * norm_w.astype(f32)
    o = o * jax.nn.silu(pz.astype(f32).reshape(B, S, GDN_HEADS, GDN_DV))
    return o.reshape(B, S, GDN_VW).astype(pq.dtype)


def masked_softmax(s, valid):
    return jax.nn.softmax(jnp.where(valid, s, NEG_INF), axis=-1) * valid


def nsa_branch(nq, kc, vc, ksl, vsl, kw, vw, gate_logits, cmp_pos, cmp_w1, cmp_w2):
    B, S, _ = nq.shape
    G, HPG = NSA_KV_GROUPS, NSA_HEADS // NSA_KV_GROUPS
    f32 = jnp.float32
    q = nq.reshape(B, S, G, HPG, NSA_DK) * NSA_DK ** -0.5
    n_cmp = (S - CMP_BLOCK) // CMP_STRIDE + 1
    cmp_start = jnp.arange(n_cmp) * CMP_STRIDE
    cmp_end = cmp_start + CMP_BLOCK - 1
    gather_idx = cmp_start[:, None] + jnp.arange(CMP_BLOCK)[None, :]

    def compress(t, pos, w1, w2):
        blocks = t[:, gather_idx] + pos[:, None, :]
        hid = jax.nn.silu(jnp.einsum('bjrgd,rde->bjge', blocks, w1))
        return jnp.einsum('bjge,ef->bjgf', hid, w2)

    k_cmp = compress(kc.reshape(B, S, G, NSA_DK), cmp_pos[0], cmp_w1[0], cmp_w2[0])
    v_cmp = compress(vc.reshape(B, S, G, NSA_DV), cmp_pos[1], cmp_w1[1], cmp_w2[1])
    n_slc = S // SLC_BLOCK
    top_n = min(SLC_TOPK, n_slc)

    def to_blocks(t):
        return t.reshape(B, n_slc, SLC_BLOCK, G, -1).transpose(0, 3, 1, 2, 4).reshape(B, G, n_slc, -1)

    k_blk = to_blocks(ksl.reshape(B, S, G, NSA_DK))
    v_blk = to_blocks(vsl.reshape(B, S, G, NSA_DV))
    slc_start = jnp.arange(n_slc) * SLC_BLOCK
    cmp_to_slc = ((cmp_end[:, None] >= slc_start[None, :]) &
                  (cmp_start[:, None] <= slc_start[None, :] + SLC_BLOCK - 1)).astype(f32)
    pad = ((0, 0), (WINDOW, 0), (0, 0), (0, 0))
    k_win = jnp.pad(kw.reshape(B, S, G, NSA_DK), pad)
    v_win = jnp.pad(vw.reshape(B, S, G, NSA_DV), pad)
    gates = jax.nn.sigmoid(gate_logits.astype(f32)).reshape(B, S, 3, G, HPG)
    slopes = alibi_slopes(NSA_HEADS).reshape(G, HPG)
    TQ = NSA_Q_BLOCK

    def attend_block(i):
        t0 = i * TQ
        t = t0 + jnp.arange(TQ)
        qb = lax.dynamic_slice_in_dim(q, t0, TQ, axis=1)
        dist = (t[:, None] - cmp_end[None, :]).astype(f32)
        valid = (dist >= 0)[None, :, None, None, :]
        s = jnp.einsum('btghd,bjgd->btghj', qb, k_cmp, preferred_element_type=f32)
        s = s - slopes[:, :, None] * dist[:, None, None, :]
        p_cmp = masked_softmax(s, valid)
        o_cmp = jnp.einsum('btghj,bjgd->btghd', p_cmp.astype(v_cmp.dtype), v_cmp)
        imp = jnp.einsum('btghj,jn->btgn', p_cmp, cmp_to_slc)
        cur = (t // SLC_BLOCK)[:, None]
        blk = jnp.arange(n_slc)[None, :]
        allowed = (blk * SLC_BLOCK <= t[:, None])[None, :, None, :]
        forced = ((blk == 0) | (blk == cur) | (blk == cur - 1))[None, :, None, :]
        score = jnp.where(forced, FORCED_SCORE, jnp.where(allowed, imp, NEG_INF))
        top_val, top_idx = lax.top_k(score, top_n)
        flat_idx = top_idx.transpose(0, 2, 1, 3).reshape(B, G, TQ * top_n, 1)
        ks = jnp.take_along_axis(k_blk, flat_idx, axis=2).reshape(B, G, TQ, top_n, SLC_BLOCK, NSA_DK)
        vs = jnp.take_along_axis(v_blk, flat_idx, axis=2).reshape(B, G, TQ, top_n, SLC_BLOCK, NSA_DV)
        pos = top_idx[..., None] * SLC_BLOCK + jnp.arange(SLC_BLOCK)
        dist = (t[None, :, None, None, None] - pos).astype(f32)
        valid = ((top_val > 0.5 * NEG_INF)[..., None] & (dist >= 0))[:, :, :, None]
        s = jnp.einsum('btghd,bgtnld->btghnl', qb, ks, preferred_element_type=f32)
        s = s - slopes[None, None, :, :, None, None] * dist[:, :, :, None]
        p = masked_softmax(s.reshape(B, TQ, G, HPG, -1), valid.reshape(B, TQ, G, 1, -1)).reshape(s.shape)
        o_slc = jnp.einsum('btghnl,bgtnld->btghd', p.astype(vs.dtype), vs)
        ks_w = lax.dynamic_slice_in_dim(k_win, t0, TQ + WINDOW, axis=1)
        vs_w = lax.dynamic_slice_in_dim(v_win, t0, TQ + WINDOW, axis=1)
        pos = t0 - WINDOW + jnp.arange(TQ + WINDOW)
        dist = (t[:, None] - pos[None, :]).astype(f32)
        valid = ((pos[None, :] >= 0) & (dist >= 0) & (dist < WINDOW))[None, :, None, None, :]
        s = jnp.einsum('btghd,bsgd->btghs', qb, ks_w, preferred_element_type=f32)
        s = s - slopes[:, :, None] * dist[:, None, None, :]
        p = masked_softmax(s, valid)
        o_win = jnp.einsum('btghs,bsgd->btghd', p.astype(vs_w.dtype), vs_w)
        g = lax.dynamic_slice_in_dim(gates, t0, TQ, axis=1)[..., None]
        return (g[:, :, 0] * o_cmp + g[:, :, 1] * o_slc + g[:, :, 2] * o_win).astype(nq.dtype)

    out = lax.map(attend_block, jnp.arange(S // TQ))
    return out.transpose(1, 0, 2, 3, 4, 5).reshape(B, S, NSA_OW)


def hybrid_mixer(u, w_in, conv_w, a_log, dt_bias, norm_w, cmp_pos, cmp_w1, cmp_w2, w_read_a, w_read_b, w_out):
    proj = jnp.einsum('bsd,dn->bsn', u, w_in)
    (pq, pk, pv, pz, pb, pa, nq, kc, vc, ksl, vsl, kw, vw, ng, ma, mb) = jnp.split(proj, _split_points(), axis=-1)
    o_a = gated_deltanet_branch(pq, pk, pv, pz, pb, pa, conv_w, a_log, dt_bias, norm_w)
    o_b = nsa_branch(nq, kc, vc, ksl, vsl, kw, vw, ng, cmp_pos, cmp_w1, cmp_w2)
    y = (jax.nn.sigmoid(ma) * jnp.einsum('bsk,kd->bsd', o_a, w_read_a) +
         jax.nn.sigmoid(mb) * jnp.einsum('bsk,kd->bsd', o_b, w_read_b))
    return jnp.einsum('bsd,de->bse', y, w_out)


def setup_inputs(seed: int = 0) -> dict:
    key = jax.random.key(seed)
    ks = iter(jax.random.split(key, 32))
    f32 = jnp.float32
    L, D = DEPTH, D_MODEL

    def nrm(shape, scale):
        return jax.random.normal(next(ks), shape, f32) * scale

    col_scale = jnp.concatenate([jnp.full((n,), DN_BETA if is_v else 1.0, f32)
                                 for n, is_v in zip(IN_SIZES, IN_IS_VALUE)])
    dt = jnp.exp(jax.random.uniform(next(ks), (L, GDN_HEADS), f32, math.log(1e-3), math.log(1e-1)))
    a_init = jax.random.uniform(next(ks), (L, GDN_HEADS), f32, 1.0, 16.0)
    return {
        'x': nrm((BATCH, SEQ, D), 1.0),
        'c': nrm((BATCH, D), 1.0),
        'ada_w': nrm((D, N_ADA * D), 0.1 * D ** -0.5),
        'ada_b': nrm((N_ADA * D,), 0.02),
        'ada_table': nrm((L, N_ADA, D), 0.02),
        'w_in': nrm((L, D, N_IN), D ** -0.5) * col_scale,
        'gdn_conv_w': nrm((L, GDN_CONV, 2 * GDN_QK + GDN_VW), GDN_CONV ** -0.5),
        'gdn_a_log': jnp.log(a_init),
        'gdn_dt_bias': dt + jnp.log(-jnp.expm1(-dt)),
        'gdn_norm_w': 1.0 + nrm((L, GDN_DV), 0.02),
        'cmp_pos': nrm((L, 2, CMP_BLOCK, NSA_DK), 0.02),
        'cmp_w1': nrm((L, 2, CMP_BLOCK, NSA_DK, NSA_DK), (CMP_BLOCK * NSA_DK) ** -0.5),
        'cmp_w2': nrm((L, 2, NSA_DK, NSA_DK), NSA_DK ** -0.5),
        'w_read_a': nrm((L, GDN_VW, D), DN_BETA * GDN_VW ** -0.5),
        'w_read_b': nrm((L, NSA_OW, D), DN_BETA * NSA_OW ** -0.5),
        'w_out': nrm((L, D, D), DN_BETA * D ** -0.5),
        'ln1_g': 1.0 + nrm((L, D), 0.02),
        'ln1_b': nrm((L, D), 0.02),
        'mlp_w1': nrm((L, D, D_FF), DN_BETA * D ** -0.5),
        'mlp_b1': nrm((L, D_FF), 0.02),
        'mlp_w2': nrm((L, D_FF, D), DN_BETA * D_FF ** -0.5),
        'mlp_b2': nrm((L, D), 0.02),
        'ln2_g': 1.0 + nrm((L, D), 0.02),
        'ln2_b': nrm((L, D), 0.02),
    }


def reference(x, c, ada_w, ada_b, ada_table, w_in, gdn_conv_w, gdn_a_log, gdn_dt_bias, gdn_norm_w,
              cmp_pos, cmp_w1, cmp_w2, w_read_a, w_read_b, w_out, ln1_g, ln1_b,
              mlp_w1, mlp_b1, mlp_w2, mlp_b2, ln2_g, ln2_b):
    B, S, D = x.shape
    mod = jnp.einsum('bd,de->be', jax.nn.silu(c), ada_w) + ada_b
    for l in range(DEPTH):
        m = (mod + ada_table[l].reshape(-1)).reshape(B, N_ADA, 1, D)
        shift1, scale1, gate1, shift2, scale2, gate2 = (m[:, j] for j in range(N_ADA))
        u = x * (1 + scale1) + shift1
        h = hybrid_mixer(u, w_in[l], gdn_conv_w[l], gdn_a_log[l], gdn_dt_bias[l], gdn_norm_w[l],
                         cmp_pos[l], cmp_w1[l], cmp_w2[l], w_read_a[l], w_read_b[l], w_out[l])
        x = layer_norm(DN_ALPHA * x + (1 + gate1) * h, ln1_g[l], ln1_b[l])
        u = x * (1 + scale2) + shift2
        h = jnp.square(jax.nn.relu(jnp.einsum('bsd,df->bsf', u, mlp_w1[l]) + mlp_b1[l]))
        h = jnp.einsum('bsf,fd->bsd', h, mlp_w2[l]) + mlp_b2[l]
        x = layer_norm(DN_ALPHA * x + (1 + gate2) * h, ln2_g[l], ln2_b[l])
    return x
```

```python
import numpy as np
from contextlib import ExitStack
import concourse.bass as bass
import concourse.mybir as mybir
from concourse.bass_utils import run_bass_kernel_spmd

F32 = mybir.dt.float32
BF16 = mybir.dt.bfloat16
AF = mybir.ActivationFunctionType
ALU = mybir.AluOpType
AX = mybir.AxisListType

NCORES = 8


class Buf:
    def __init__(self, prog, name, t, space):
        self.prog = prog
        self.name = name
        self.t = t
        self.space = space
        self.last_w = None
        self.readers = []
        self.dsem = None
        self.dcount = 0

    def __getitem__(self, idx):
        return self.t[idx]


class Prog:
    ENGS = ("pe", "act", "dve", "pool", "sp")

    def __init__(self, nc):
        self.nc = nc
        self.es = ExitStack()
        self.scopes = [self.es]
        self.lists = {e: [] for e in self.ENGS}
        self.sems = {}
        self.counts = {e: 0 for e in self.ENGS}
        for e in ("pe", "act", "dve", "pool"):
            self.sems[e] = self.es.enter_context(nc.semaphore("s_" + e))
        self.waited = {e: {} for e in self.ENGS}
        self.dma_bufs = []
        self.uid = 0
        self.free_dsems = {"sp": [], "pool": []}
        self.scope_bufs = [[]]

    def push(self):
        s = ExitStack()
        self.scopes.append(s)
        self.scope_bufs.append([])

    def pop(self):
        self.barrier()
        for b in self.scope_bufs.pop():
            self.free_dsems[b.dq].append((b.dsem, b.dcount))
        self.scopes.pop().close()

    def sb(self, name, shape, dt=F32):
        self.uid += 1
        t = self.scopes[-1].enter_context(self.nc.sbuf_tensor(f"{name}_{self.uid}", list(shape), dt))
        return Buf(self, f"{name}_{self.uid}", t, "sb")

    def ps(self, name, shape, dt=F32):
        self.uid += 1
        t = self.scopes[-1].enter_context(self.nc.psum_tensor(f"{name}_{self.uid}", list(shape), dt))
        return Buf(self, f"{name}_{self.uid}", t, "ps")

    def _need(self, eng, dep, lst):
        if dep is None:
            return
        if dep[0] == "dma":
            _, buf, cnt = dep
            key = ("dma", id(buf))
            sem, val = buf.dsem, cnt
        else:
            e, cnt = dep
            if e == eng and e == "pe":
                return
            key = e
            sem, val = self.sems[e], cnt
        if self.waited[eng].get(key, 0) >= val:
            return
        self.waited[eng][key] = val
        lst.append((sem, val))

    def op(self, eng, fn, reads=(), writes=()):
        waits = []
        reads = [getattr(b, "buf", b) for b in reads]
        writes = [getattr(b, "buf", b) for b in writes]
        for b in reads:
            self._need(eng, b.last_w, waits)
        for b in writes:
            self._need(eng, b.last_w, waits)
            for r in b.readers:
                self._need(eng, r, waits)
        self.counts[eng] += 1
        me = (eng, self.counts[eng])
        for b in reads:
            b.readers.append(me)
            if len(b.readers) > 40:
                last = {}
                for r in b.readers:
                    k = r[0] if r[0] != "dma" else ("dma", id(r[1]))
                    last[k] = r
                b.readers = list(last.values())
        for b in writes:
            b.last_w = me
            b.readers = []
        self.lists[eng].append((waits, fn, (self.sems[eng], 1)))

    def dma(self, out_ap, in_ap, reads=(), writes=(), q="sp", **kw):
        waits = []
        reads = [getattr(b, "buf", b) for b in reads]
        writes = [getattr(b, "buf", b) for b in writes]
        bufs = list(reads) + list(writes)
        assert len(bufs) == 1
        b = bufs[0]
        self._need(q, b.last_w, waits)
        if writes:
            for r in b.readers:
                self._need(q, r, waits)
        if b.dsem is None:
            b.dq = q
            if self.free_dsems[q]:
                b.dsem, b.dcount = self.free_dsems[q].pop()
            else:
                b.dsem = self.es.enter_context(self.nc.semaphore("d_" + b.name))
            self.scope_bufs[-1].append(b)
        assert b.dq == q, (b.name, b.dq, q)
        if not any(x is b for x in self.dma_bufs):
            self.dma_bufs.append(b)
        b.dcount += 16
        me = ("dma", b, b.dcount)
        if reads:
            b.readers.append(me)
        else:
            b.last_w = me
            b.readers = []
        fn = lambda e, o=out_ap, i=in_ap, kw=kw: e.dma_start(out=o, in_=i, **kw)
        self.lists[q].append((waits, fn, (b.dsem, 16)))

    def barrier(self):
        for e in self.ENGS:
            waits = []
            for o in ("pe", "act", "dve", "pool"):
                if o != e and self.counts[o] > 0:
                    self._need(e, (o, self.counts[o]), waits)
            for b in self.dma_bufs:
                self._need(e, ("dma", b, b.dcount), waits)
            if waits:
                self.lists[e].append((waits, None, None))
        self.dma_bufs = []

    def finish(self):
        self.barrier()
        nc = self.nc
        engmap = {"pe": "tensor", "act": "scalar", "dve": "vector", "pool": "gpsimd", "sp": "sync"}
        with nc.Block() as block:
            for ename, attr in engmap.items():
                lst = self.lists[ename]

                def body(e, lst=lst):
                    for waits, fn, inc in lst:
                        for sem, val in waits:
                            e.wait_ge(sem, val)
                        if fn is not None:
                            fn(e).then_inc(inc[0], inc[1])

                getattr(block, attr)(body)
        while self.scopes:
            self.scopes.pop().close()

    def mm(self, out, lhsT, rhs, start=True, stop=True):
        self.op("pe", lambda e: e.matmul(out[1], lhsT[1], rhs[1], start=start, stop=stop),
                reads=[lhsT[0], rhs[0]], writes=[out[0]])

    def tr(self, out, in_, ident):
        self.op("pe", lambda e: e.matmul(out[1], in_[1], ident[1], start=True, stop=True),
                reads=[in_[0], ident[0]], writes=[out[0]])

    def act(self, out, in_, func, bias=None, scale=None, accum=None, eng="act"):
        kw = {}
        rd = [in_[0]]
        wr = [out[0]]
        if bias is not None:
            if isinstance(bias, tuple):
                kw["bias"] = bias[1]; rd.append(bias[0])
            else:
                kw["bias"] = bias
        if scale is not None:
            if isinstance(scale, tuple):
                kw["scale"] = scale[1]; rd.append(scale[0])
            else:
                kw["scale"] = scale
        if accum is not None:
            kw["accum_out"] = accum[1]; wr.append(accum[0])
        self.op("act", lambda e: e.activation(out=out[1], in_=in_[1], func=func, **kw), reads=rd, writes=wr)

    def tt(self, out, in0, in1, op, eng="dve"):
        self.op(eng, lambda e: e.tensor_tensor(out=out[1], in0=in0[1], in1=in1[1], op=op),
                reads=[in0[0], in1[0]], writes=[out[0]])

    def ts(self, out, in0, s1, s2, op0, op1=None, eng="dve", accum=None):
        rd = [in0[0]]
        wr = [out[0]]
        a1 = s1
        if isinstance(s1, tuple):
            a1 = s1[1]; rd.append(s1[0])
        a2 = s2
        if isinstance(s2, tuple):
            a2 = s2[1]; rd.append(s2[0])
        kw = {}
        if op1 is not None:
            kw["op1"] = op1
        if accum is not None:
            kw["accum_out"] = accum[1]; wr.append(accum[0])
        self.op(eng, lambda e: e.tensor_scalar(out=out[1], in0=in0[1], scalar1=a1, scalar2=a2, op0=op0, **kw),
                reads=rd, writes=wr)

    def stt(self, out, in0, s, in1, op0, op1, eng="dve"):
        rd = [in0[0], in1[0]]
        a = s
        if isinstance(s, tuple):
            a = s[1]; rd.append(s[0])
        self.op(eng, lambda e: e.scalar_tensor_tensor(out=out[1], in0=in0[1], scalar=a, in1=in1[1], op0=op0, op1=op1),
                reads=rd, writes=[out[0]])

    def cp(self, out, in_, eng="dve"):
        if eng == "act":
            self.op("act", lambda e: e.copy(out=out[1], in_=in_[1]), reads=[in_[0]], writes=[out[0]])
        else:
            self.op(eng, lambda e: e.tensor_copy(out=out[1], in_=in_[1]), reads=[in_[0]], writes=[out[0]])

    def rsqrt(self, out, in_, eps, scale):
        self.act(out, in_, AF.Sqrt, bias=eps, scale=scale)
        self.op("dve", lambda e: e.reciprocal(out=out[1], in_=out[1]), reads=[out[0]], writes=[out[0]])

    def memset(self, out, val, eng="dve"):
        self.op(eng, lambda e: e.memset(out[1], val), writes=[out[0]])


def _run(nc, in_maps):
    res = run_bass_kernel_spmd(nc, in_maps, core_ids=list(range(len(in_maps))))
    return res.results


D = 4096
KC = D // 128
NADA = 6 * D
MOD_COLS = NADA // NCORES


def build_mod():
    nc = bass.Bass("TRN2", target_bir_lowering=False)
    cT = nc.dram_tensor("cT", [128, KC * 4], F32, kind="ExternalInput").ap()
    w = nc.dram_tensor("w", [D, MOD_COLS], F32, kind="ExternalInput").ap()
    bias = nc.dram_tensor("b", [4, MOD_COLS], F32, kind="ExternalInput").ap()
    out = nc.dram_tensor("mod", [4, MOD_COLS], F32, kind="ExternalOutput").ap()
    p = Prog(nc)
    c_sb = p.sb("c_sb", [128, KC * 4])
    cs_sb = p.sb("cs_sb", [128, KC * 4])
    b_sb = p.sb("b_sb", [4, MOD_COLS])
    o_sb = p.sb("o_sb", [4, MOD_COLS])
    NW = 4
    wb = [p.sb(f"wb{i}", [128, 8, 512]) for i in range(NW)]
    pss = [p.ps(f"ps{i}", [128, 512]) for i in range(2)]
    p.dma(c_sb[:, :], cT, writes=[c_sb])
    p.dma(b_sb[:, :], bias, writes=[b_sb])
    p.op("act", lambda e: e.activation(out=cs_sb[:, :], in_=c_sb[:, :], func=AF.Silu),
         reads=[c_sb], writes=[cs_sb])
    wv = w.rearrange("(kc p) n -> p kc n", p=128)
    it = 0
    for ct in range(MOD_COLS // 512):
        ps = pss[ct % 2]
        for kg in range(KC // 8):
            wbuf = wb[it % NW]
            it += 1
            p.dma(wbuf[:, :, :], wv[:, kg * 8:(kg + 1) * 8, ct * 512:(ct + 1) * 512], writes=[wbuf])
            for k8 in range(8):
                kc = kg * 8 + k8
                p.op("pe", lambda e, ps=ps, kc=kc, wbuf=wbuf, k8=k8: e.matmul(
                    ps[0:4, :], cs_sb[:, kc * 4:(kc + 1) * 4], wbuf[:, k8, :],
                    start=(kc == 0), stop=(kc == KC - 1)),
                    reads=[cs_sb, wbuf], writes=[ps])
        p.op("dve", lambda e, ps=ps, ct=ct: e.tensor_tensor(
            out=o_sb[:, ct * 512:(ct + 1) * 512], in0=ps[0:4, :],
            in1=b_sb[:, ct * 512:(ct + 1) * 512], op=ALU.add),
            reads=[ps, b_sb], writes=[o_sb])
    p.dma(out, o_sb[:, :], reads=[o_sb])
    p.finish()
    return nc


def run_mod(c, ada_w, ada_b):
    cT = np.ascontiguousarray(c.T.reshape(KC, 128, 4).transpose(1, 0, 2).reshape(128, KC * 4))
    in_maps = []
    for i in range(NCORES):
        sl = slice(i * MOD_COLS, (i + 1) * MOD_COLS)
        in_maps.append({"cT": cT, "w": np.ascontiguousarray(ada_w[:, sl]),
                        "b": np.ascontiguousarray(np.broadcast_to(ada_b[sl], (4, MOD_COLS)))})
    res = _run(build_mod(), in_maps)
    return np.concatenate([r["mod"] for r in res], axis=1)


class Cfg:
    def __init__(self, D=4096, S=4096, H=16, G=4, DFF=16384, L=2):
        self.D, self.S, self.H, self.G, self.DFF, self.L = D, S, H, G, DFF, L
        self.KC = D // 128
        self.NH = 4 * G
        segs = [("gq", H * 128), ("gk", H * 128), ("gv", H * 128), ("gz", H * 128),
                ("ma", D), ("mb", D),
                ("nq", self.NH * 128), ("kc", G * 128), ("vc", G * 128), ("ksl", G * 128),
                ("vsl", G * 128), ("kw", G * 128), ("vw", G * 128)]
        self.off = {}
        o = 0
        for n, w in segs:
            self.off[n] = o
            o += w
        self.NF = self.off["nq"]
        self.NB = o - self.NF
        self.NBIG = o
        self.NSM = 2 * H + 3 * self.NH
        self.NCOLS = o + self.NSM


def win_perm(C):
    H, G, D = C.H, C.G, C.D
    sizes = (H * 128, H * 128, H * 128, H * 128, H, H, C.NH * 128, G * 128, G * 128, G * 128,
             G * 128, G * 128, G * 128, 3 * C.NH, D, D)
    names = ("gq", "gk", "gv", "gz", "pb", "pa", "nq", "kc", "vc", "ksl", "vsl", "kw", "vw", "ng", "ma", "mb")
    st = {}
    o = 0
    for n, s in zip(names, sizes):
        st[n] = (o, s)
        o += s
    order = ["gq", "gk", "gv", "gz", "ma", "mb", "nq", "kc", "vc", "ksl", "vsl", "kw", "vw", "pb", "pa", "ng"]
    return np.concatenate([np.arange(st[n][0], st[n][0] + st[n][1]) for n in order])


def stage_proj(p, C, x_tok, w, w_s, mods, projF, projB, small, cst):
    D, S, KC = C.D, C.S, C.KC
    TT = min(1024, S)
    CT = 256
    p.push()
    uT = p.sb("uT", [128, KC * TT], BF16)
    wf = [p.sb(f"wf{i}", [128, max(KC * CT, D)]) for i in range(2)]
    wb = [p.sb(f"wb{i}", [128, KC * CT], BF16) for i in range(2)]
    ost = [p.sb(f"ost{i}", [128, 512]) for i in range(4)]
    ostb = [p.sb(f"ostb{i}", [128, 512], BF16) for i in range(4)]
    pss = [p.ps(f"pp{i}", [128, 512]) for i in range(8)]
    sc, sh = mods
    ident = cst["ident"]
    nps = 0
    nw = 0
    no = 0
    for tt0 in range(0, S, TT):
        for kc in range(KC):
            xb = wf[nw % 2]
            nw += 1
            p.dma(xb[:, 0:TT], x_tok[kc * 128:(kc + 1) * 128, tt0:tt0 + TT], writes=[xb])
            p.ts((uT, uT[:, kc * TT:(kc + 1) * TT]), (xb, xb[:, 0:TT]),
                 (sc, sc[:, kc:kc + 1]), (sh, sh[:, kc:kc + 1]), ALU.mult, ALU.add)
        for c0 in range(0, C.NBIG, CT):
            cw = min(CT, C.NBIG - c0)
            wfb = wf[nw % 2]
            wbb = wb[nw % 2]
            nw += 1
            wfv = wfb[:, 0:KC * cw].rearrange("p (k n) -> p k n", n=cw)
            assert cw == CT
            p.dma(wfv, w[c0 // CT], writes=[wfb])
            half = (KC * cw) // 2
            p.cp((wbb, wbb[:, 0:half]), (wfb, wfb[:, 0:half]), eng="dve")
            p.cp((wbb, wbb[:, half:KC * cw]), (wfb, wfb[:, half:KC * cw]), eng="pool")
            for cb in range(0, cw, 128):
                col = c0 + cb
                for ts0 in range(0, TT, 512):
                    tw = min(512, TT - ts0)
                    ps = pss[nps % 8]
                    nps += 1
                    for kc in range(KC):
                        p.mm((ps, ps[:, 0:tw]), (wbb, wbb[:, kc * cw + cb: kc * cw + cb + 128]),
                             (uT, uT[:, kc * TT + ts0: kc * TT + ts0 + tw]), start=(kc == 0), stop=(kc == KC - 1))
                    no += 1
                    if col < C.NF:
                        o = ost[no % 4]
                        p.cp((o, o[:, 0:tw]), (ps, ps[:, 0:tw]), eng="act")
                        p.dma(projF[col:col + 128, tt0 + ts0: tt0 + ts0 + tw], o[:, 0:tw], reads=[o], q="pool")
                    else:
                        o = ostb[no % 4]
                        scl = 128.0 ** -0.5 if col < C.off["kc"] else 1.0
                        p.act((o, o[:, 0:tw]), (ps, ps[:, 0:tw]), AF.Copy, scale=scl)
                        p.dma(projB[col - C.NF: col - C.NF + 128, tt0 + ts0: tt0 + ts0 + tw], o[:, 0:tw],
                              reads=[o], q="pool")
        nsm = C.NSM
        wfb = wf[nw % 2]
        wbb = wb[nw % 2]
        nw += 1
        wfv = wfb[:, 0:KC * nsm].rearrange("p (k n) -> p k n", n=nsm)
        p.dma(wfv, w_s, writes=[wfb])
        p.cp((wbb, wbb[:, 0:KC * nsm]), (wfb, wfb[:, 0:KC * nsm]), eng="dve")
        for tb in range(TT // 128):
            ps = pss[nps % 8]
            nps += 1
            for kc in range(KC):
                p.mm((ps, ps[:, 0:nsm]), (uT, uT[:, kc * TT + tb * 128: kc * TT + (tb + 1) * 128]),
                     (wbb, wbb[:, kc * nsm:(kc + 1) * nsm]), start=(kc == 0), stop=(kc == KC - 1))
            no += 1
            o = ost[no % 4]
            p.cp((o, o[:, 0:nsm]), (ps, ps[:, 0:nsm]), eng="act")
            p.dma(small[tt0 + tb * 128: tt0 + (tb + 1) * 128, :], o[:, 0:nsm], reads=[o], q="pool")
    p.pop()


def load_consts(p, nc, C, tag=""):
    cst = {}
    d = nc.dram_tensor("cst_f32", [128, 6 * 128], F32, kind="ExternalInput").ap()
    t = p.sb("cstf", [128, 6 * 128])
    p.dma(t[:, :], d, writes=[t])
    names = ["ident", "ones", "tri", "maskl", "masku", "maskdiag"]
    for i, n in enumerate(names):
        cst[n] = View(t, t[:, i * 128:(i + 1) * 128])
    tb = p.sb("cstb", [128, 128], BF16)
    p.cp((tb, tb[:, :]), (t, t[:, 0:128]))
    cst["identb"] = View(tb, tb[:, :])
    return cst


class View:
    def __init__(self, buf, ap):
        self.buf = buf
        self.ap = ap

    def __getitem__(self, idx):
        return self.ap[idx]


def consts_np():
    i = np.arange(128)
    ident = np.eye(128, dtype=np.float32)
    ones = np.ones((128, 128), np.float32)
    tri = (i[:, None] <= i[None, :]).astype(np.float32)
    maskl = np.where(i[None, :] < i[:, None], 0.0, -1e30).astype(np.float32)
    masku = np.where(i[:, None] <= i[None, :], 0.0, 1e30).astype(np.float32)
    maskd = (i[:, None] != i[None, :]).astype(np.float32)
    return np.concatenate([ident, ones, tri, maskl, masku, maskd], axis=1)


def bc_mid(view_ap_tile, off, rowlen, n_mid, n_in):
    return bass.AP(view_ap_tile, off, [[rowlen, 128], [0, n_mid], [1, n_in]])


def stage_gdn(p, C, projF, small, gpar, oaT, cst):
    S, H = C.S, C.H
    NCH = S // 128
    NSM = C.NSM
    ident, ones, tri, maskl, masku = cst["ident"], cst["ones"], cst["tri"], cst["maskl"], cst["masku"]
    convw, alog, dtb, normw = gpar
    p.push()
    pss = [p.ps(f"gp{i}", [128, 512]) for i in range(8)]
    nps = [0]

    def PS():
        nps[0] += 1
        return pss[nps[0] % 8]

    SM = p.sb("SM", [128, NCH * NSM])
    p.dma(SM[:, :].rearrange("p (c n) -> p c n", n=NSM), small.rearrange("(c p) n -> p c n", p=128), writes=[SM])
    cw = p.sb("cw", [128, 3 * H * 4]); p.dma(cw[:, :], convw, writes=[cw])
    al = p.sb("al", [128, H]); p.dma(al[:, :], alog, writes=[al])
    db = p.sb("db", [128, H]); p.dma(db[:, :], dtb, writes=[db])
    nw = p.sb("nw", [128, 1]); p.dma(nw[:, :], normw, writes=[nw])
    nea = p.sb("nea", [128, H])
    p.act((nea, nea[:, :]), (al, al[:, :]), AF.Exp)
    p.ts((nea, nea[:, :]), (nea, nea[:, :]), -1.0, None, ALU.mult)
    SM3 = SM[:, :].rearrange("p (c n) -> p c n", n=NSM)
    BETA = p.sb("BETA", [128, NCH * H]); LA = p.sb("LA", [128, NCH * H])
    EG = p.sb("EG", [128, NCH * H]); EGL = p.sb("EGL", [128, NCH * H])
    BG = p.sb("BG", [128, NCH * H]); KE = p.sb("KE", [128, NCH * H])
    v3 = lambda t: t[:, :].rearrange("p (c n) -> p c n", n=H)
    p.act((BETA, v3(BETA)), (SM, SM3[:, :, 0:H]), AF.Sigmoid)
    p.tt((LA, v3(LA)), (SM, SM3[:, :, H:2 * H]), (db, bc_mid(db.t, 0, H, NCH, H)), ALU.add)
    p.act((LA, LA[:, :]), (LA, LA[:, :]), AF.Exp)
    p.act((LA, LA[:, :]), (LA, LA[:, :]), AF.Ln, bias=1.0)
    p.tt((LA, v3(LA)), (LA, v3(LA)), (nea, bc_mid(nea.t, 0, H, NCH, H)), ALU.mult)
    for c in range(NCH):
        sl = slice(c * H, (c + 1) * H)
        pg = PS(); pl = PS()
        p.mm((pg, pg[:, 0:H]), (tri, tri[:, :]), (LA, LA[:, sl]))
        p.mm((pl, pl[:, 0:H]), (ones, ones[:, :]), (LA, LA[:, sl]))
        p.act((EG, EG[:, sl]), (pg, pg[:, 0:H]), AF.Exp)
        p.act((EGL, EGL[:, sl]), (pl, pl[:, 0:H]), AF.Exp)
        p.cp((BG, BG[:, sl]), (pg, pg[:, 0:H]), eng="act")
        p.cp((KE, KE[:, sl]), (pl, pl[:, 0:H]), eng="act")
        p.tt((KE, KE[:, sl]), (KE, KE[:, sl]), (BG, BG[:, sl]), ALU.subtract)
        p.act((KE, KE[:, sl]), (KE, KE[:, sl]), AF.Exp)
        p.tt((BG, BG[:, sl]), (BETA, BETA[:, sl]), (EG, EG[:, sl]), ALU.mult)

    import os as _os
    STOP = int(_os.environ.get("GDN_STOP", "99"))
    if STOP <= 1:
        p.pop(); return
    raw = [p.sb(f"raw{i}", [128, 4 + S]) for i in range(3)]
    cv = [p.sb(f"cv{i}", [128, S]) for i in range(3)]
    zt = p.sb("zt", [128, S])
    rs = p.sb("rs", [128, 512])
    oast = p.sb("oast", [128, S], BF16)
    Sst = p.sb("Sst", [128, 128])
    T = {n: p.sb(n, [128, 128]) for n in
         ("latri", "nlatri", "tmpL", "tmpU", "decL", "decU", "Lm", "qkT", "R", "P0", "P1", "PT1", "PT0",
          "Bk", "kend", "Bv", "WT", "Uu", "ucorr", "o1", "o", "on", "sq")}
    ssq = p.sb("ssq", [128, 2])
    for i in range(3):
        p.memset((raw[i], raw[i][:, 0:4]), 0.0)
    for h in range(H):
        for i, seg in enumerate(("gq", "gk", "gv")):
            r0 = C.off[seg] + h * 128
            p.dma(raw[i][:, 4:4 + S], projF[r0:r0 + 128, :], writes=[raw[i]])
            wcol = lambda j, i=i: (cw, cw[:, (i * H + h) * 4 + j:(i * H + h) * 4 + j + 1])
            e = "dve"
            p.ts((cv[i], cv[i][:, :]), (raw[i], raw[i][:, 1:1 + S]), wcol(0), None, ALU.mult, eng=e)
            for j in (1, 2, 3):
                p.stt((cv[i], cv[i][:, :]), (raw[i], raw[i][:, 1 + j:1 + j + S]), wcol(j), (cv[i], cv[i][:, :]),
                      ALU.mult, ALU.add, eng=e)
            p.act((cv[i], cv[i][:, :]), (cv[i], cv[i][:, :]), AF.Silu)
        r0 = C.off["gz"] + h * 128
        p.dma(zt[:, :], projF[r0:r0 + 128, :], writes=[zt])
        p.act((zt, zt[:, :]), (zt, zt[:, :]), AF.Silu)
        for i in (0, 1):
            sqt = raw[i]
            p.act((sqt, sqt[:, 4:4 + S]), (cv[i], cv[i][:, :]), AF.Square)
            for b0 in range(0, S, 512):
                bw = min(512, S - b0)
                ps = PS()
                p.mm((ps, ps[:, 0:bw]), (ones, ones[:, :]), (sqt, sqt[:, 4 + b0:4 + b0 + bw]))
                p.rsqrt((rs, rs[:, 0:bw]), (ps, ps[:, 0:bw]), 1e-6, 1.0)
                if i == 0:
                    p.stt((cv[i], cv[i][:, b0:b0 + bw]), (cv[i], cv[i][:, b0:b0 + bw]), 128.0 ** -0.5,
                          (rs, rs[:, 0:bw]), ALU.mult, ALU.mult)
                else:
                    p.tt((cv[i], cv[i][:, b0:b0 + bw]), (cv[i], cv[i][:, b0:b0 + bw]), (rs, rs[:, 0:bw]), ALU.mult)
        p.memset((Sst, Sst[:, :]), 0.0)
        for c in range(NCH if STOP > 2 else 0):
            cs = slice(c * 128, (c + 1) * 128)
            qT = (cv[0], cv[0][:, cs]); kT = (cv[1], cv[1][:, cs]); vT = (cv[2], cv[2][:, cs])
            sc_ = lambda t: (t, t[:, c * H + h:c * H + h + 1])
            A = lambda n: (T[n], T[n][:, :])
            p.ts(A("latri"), (tri, tri[:, :]), sc_(LA), None, ALU.mult, eng="pool")
            p.ts(A("nlatri"), (tri, tri[:, :]), sc_(LA), -1.0, ALU.mult, ALU.mult, eng="pool")
            pG = PS()
            p.mm((pG, pG[:, 0:128]), A("latri"), (ones, ones[:, :]), start=True, stop=False)
            p.mm((pG, pG[:, 0:128]), (ones, ones[:, :]), A("nlatri"), start=False, stop=True)
            p.tt(A("tmpL"), (pG, pG[:, 0:128]), (maskl, maskl[:, :]), ALU.add)
            p.act(A("decL"), A("tmpL"), AF.Exp)
            p.tt(A("tmpU"), (pG, pG[:, 0:128]), (masku, masku[:, :]), ALU.add)
            p.act(A("decU"), A("tmpU"), AF.Exp, scale=-1.0)
            pkk = PS(); pkq = PS()
            p.mm((pkk, pkk[:, 0:128]), kT, kT)
            p.mm((pkq, pkq[:, 0:128]), kT, qT)
            p.stt(A("Lm"), (pkk, pkk[:, 0:128]), sc_(BETA), A("decL"), ALU.mult, ALU.mult)
            p.tt(A("qkT"), (pkq, pkq[:, 0:128]), A("decU"), ALU.mult)
            if STOP <= 3:
                continue
            pU = PS()
            p.tr((pU, pU[:, 0:128]), A("Lm"), (ident, ident[:, :]))
            p.tt(A("R"), (ident, ident[:, :]), (pU, pU[:, 0:128]), ALU.subtract)
            p.cp(A("P0"), (pU, pU[:, 0:128]), eng="dve")
            Pc, PTc = "P0", "Lm"
            NIT = int(_os.environ.get("GDN_NIT", "6"))
            for it in range(NIT):
                Pn = "P1" if Pc == "P0" else "P0"
                PTn = "PT1" if PTc in ("Lm", "PT0") else "PT0"
                pa = PS()
                p.mm((pa, pa[:, 0:128]), A(Pc), A(PTc))
                if it < 5:
                    pb_ = PS()
                    p.mm((pb_, pb_[:, 0:128]), A(PTc), A(Pc))
                p.cp(A(PTn), (pa, pa[:, 0:128]), eng="act")
                if it < 5:
                    p.cp(A(Pn), (pb_, pb_[:, 0:128]), eng="dve")
                pr = PS()
                p.mm((pr, pr[:, 0:128]), A(PTn), A("R"))
                p.tt(A("R"), A("R"), (pr, pr[:, 0:128]), ALU.add)
                Pc, PTc = Pn, PTn
            if STOP <= 4:
                continue
            pK = PS(); pV = PS()
            p.tr((pK, pK[:, 0:128]), kT, (ident, ident[:, :]))
            p.tr((pV, pV[:, 0:128]), vT, (ident, ident[:, :]))
            p.ts(A("Bk"), (pK, pK[:, 0:128]), sc_(BG), None, ALU.mult)
            p.ts(A("kend"), (pK, pK[:, 0:128]), sc_(KE), None, ALU.mult)
            p.act(A("Bv"), (pV, pV[:, 0:128]), AF.Copy, scale=sc_(BETA))
            pW = PS(); pUu = PS()
            p.mm((pW, pW[:, 0:128]), A("Bk"), A("R"))
            p.mm((pUu, pUu[:, 0:128]), A("R"), A("Bv"))
            p.cp(A("WT"), (pW, pW[:, 0:128]), eng="act")
            p.cp(A("Uu"), (pUu, pUu[:, 0:128]), eng="dve")
            if STOP <= 5:
                continue
            pws = PS(); po1 = PS()
            p.mm((pws, pws[:, 0:128]), A("WT"), (Sst, Sst[:, :]))
            p.mm((po1, po1[:, 0:128]), qT, (Sst, Sst[:, :]))
            p.tt(A("ucorr"), A("Uu"), (pws, pws[:, 0:128]), ALU.subtract)
            p.act(A("o1"), (po1, po1[:, 0:128]), AF.Copy, scale=sc_(EG))
            po2 = PS(); pS = PS()
            p.mm((po2, po2[:, 0:128]), A("qkT"), A("ucorr"))
            p.mm((pS, pS[:, 0:128]), A("kend"), A("ucorr"))
            p.tt(A("o"), A("o1"), (po2, po2[:, 0:128]), ALU.add)
            p.stt((Sst, Sst[:, :]), (Sst, Sst[:, :]), sc_(EGL), (pS, pS[:, 0:128]), ALU.mult, ALU.add)
            p.act(A("sq"), A("o"), AF.Square, accum=(ssq, ssq[:, 0:1]))
            p.rsqrt((ssq, ssq[:, 1:2]), (ssq, ssq[:, 0:1]), 1e-6, 1.0 / 128)
            p.ts(A("on"), A("o"), (ssq, ssq[:, 1:2]), None, ALU.mult)
            pT_ = PS()
            p.tr((pT_, pT_[:, 0:128]), A("on"), (ident, ident[:, :]))
            p.stt((oast, oast[:, cs]), (pT_, pT_[:, 0:128]), (nw, nw[:, 0:1]), (zt, zt[:, cs]), ALU.mult, ALU.mult)
        p.dma(oaT[h * 128:(h + 1) * 128, :], oast[:, :], reads=[oast], q="pool")
    p.pop()


def nsa_consts_np(C):
    S = C.S
    NQ = S // 128
    NS = S // 64
    ncmp = S // 16 - 1
    t = np.arange(S)
    cmp_end = np.arange(256 * ((ncmp + 255) // 256)) * 16 + 31
    jj = np.arange(len(cmp_end))
    dist = (t[:, None] - cmp_end[None, :]).astype(np.float32)
    valid = (dist >= 0) & (jj[None, :] < ncmp)
    cmpd = np.where(valid, dist, 0.0).astype(np.float32).reshape(NQ, 128, -1)
    cmpm = np.where(valid, 0.0, -1e30).astype(np.float32).reshape(NQ, 128, -1)
    cmp_start = jj * 16
    slc_start = np.arange(NS) * 64
    c2s = ((cmp_end[:, None] >= slc_start[None, :]) & (cmp_start[:, None] <= slc_start[None, :] + 63)
           & (jj[:, None] < ncmp)).astype(np.float32)
    cur = (t // 64)[:, None]
    blk = np.arange(NS)[None, :]
    allowed = blk * 64 <= t[:, None]
    fb = np.where(allowed, 0.0, -1e30)
    fb = np.where(blk == cur - 1, 1e6, fb)
    fb = np.where(blk == cur, 2e6, fb)
    fb = np.where(blk == 0, 3e6, fb)
    forced = (blk == 0) | (blk == cur) | (blk == cur - 1)
    allow = (allowed & ~forced).astype(np.float32)
    tl = np.arange(128)[:, None]
    c = np.arange(S + 128)[None, :]
    relc = (tl + S - c).astype(np.float32)
    cb = np.arange(640)[None, :]
    dband = tl + 512 - cb
    band = np.where((dband >= 0) & (dband < 512), 0.0, -1e30).astype(np.float32)
    kl = np.arange(128)[None, :]
    caus = np.where(kl <= tl, 0.0, -1e30).astype(np.float32)
    return dict(cmpd=cmpd, cmpm=cmpm, c2s=c2s.astype(np.float32), allow=allow.reshape(NQ, 128, NS),
                fb=fb.astype(np.float32).reshape(NQ, 128, NS), relc=relc, band=band, caus=caus)


def stage_nsa(p, C, nc, projB, small, npar, ncst, obT, cst):
    S, G, NH, H = C.S, C.G, C.NH, C.H
    NQ = S // 128
    NS = S // 64
    NCMP = S // 16 - 1
    NJ = ncst["cmpd"].shape[-1]
    NJT = NJ // 128
    slopes = [2.0 ** (-8.0 * (h + 1) / NH) for h in range(NH)]
    pos_d, w1_d, w2_d = npar
    ident, identb = cst["ident"], cst["identb"]
    rowB = lambda seg, g: C.off[seg] - C.NF + g * 128
    p.push()
    pss = [p.ps(f"np{i}", [128, 512]) for i in range(8)]
    nps = [0]

    def PS():
        nps[0] += 1
        return pss[3 + nps[0] % 5]

    npo = [0]

    def PO():
        npo[0] += 1
        return pss[1 + npo[0] % 2]

    relc = p.sb("relc", [128, S + 128]); p.dma(relc[:, :], ncst["relc"], writes=[relc])
    band = p.sb("band", [128, 640]); p.dma(band[:, :], ncst["band"], writes=[band])
    caus = p.sb("caus", [128, 128]); p.dma(caus[:, :], ncst["caus"], writes=[caus])
    c2sf = p.sb("c2sf", [128, NJT * NS]); c2s = p.sb("c2s", [128, NJT * NS], BF16)
    p.dma(c2sf[:, :].rearrange("p (j n) -> p j n", n=NS), ncst["c2s"].rearrange("(j p) n -> p j n", p=128), writes=[c2sf])
    p.cp((c2s, c2s[:, :]), (c2sf, c2sf[:, :]))
    qT = p.sb("qT", [128, 4 * S], BF16)
    KT = {n: p.sb("KT" + n, [128, S], BF16) for n in ("kc", "vc", "ksl", "kw")}
    VT = p.sb("VTt", [128, S], BF16)
    Vtok = {n: p.sb("Vtok" + n, [128, S], BF16) for n in ("vsl", "vw")}
    w1b = p.sb("w1b", [128, 32 * 128], BF16)
    w2f = p.sb("w2f", [128, 128]); w2b = p.sb("w2b", [128, 128], BF16)
    posf = p.sb("posf", [128, 32]); posb = p.sb("posb", [128, 32], BF16)
    cvec = p.sb("cvec", [128, 1])
    hidT = p.sb("hidT", [128, NJ], BF16)
    kcmpT = p.sb("kcmpT", [128, NJ], BF16)
    vcmp = p.sb("vcmp", [128, NJT * 128], BF16)
    sc = p.sb("sc", [128, max(S, 32 * 128)]); pb_ = p.sb("pbf", [128, S], BF16)
    w1f = sc
    selb = p.sb("selb", [128, S])
    pT4 = [p.sb(f"pT4{i}", [128, 512], BF16) for i in range(2)]
    cd = p.sb("cd", [128, NJ]); cm = p.sb("cm", [128, NJ])
    alw = p.sb("alw", [128, NS]); fbt = p.sb("fbt", [128, NS])
    score = p.sb("score", [128, NS]); sc2 = p.sb("sc2", [128, NS]); m8 = p.sb("m8", [128, 16])
    selm = p.sb("selm", [128, NS])
    st = p.sb("st", [128, 8])
    gat = p.sb("gat", [128, 3 * NH])
    obh = [p.sb(f"obh{i}", [128, 128]) for i in range(4)]
    obst = p.sb("obst", [128, 4 * S], BF16)
    npt = [0]

    def softmax_pv(nk, Vt, voff, po):
        p.op("dve", lambda e: e.reduce_max(out=st[:, 0:1], in_=sc[:, 0:nk], axis=AX.X), reads=[sc], writes=[st])
        p.ts((st, st[:, 1:2]), (st, st[:, 0:1]), -1e20, -1.0, ALU.max, ALU.mult)
        p.act((pb_, pb_[:, 0:nk]), (sc, sc[:, 0:nk]), AF.Exp, bias=(st, st[:, 1:2]), accum=(st, st[:, 2:3]))
        p.ts((st, st[:, 3:4]), (st, st[:, 2:3]), 1e-30, None, ALU.add)
        p.op("dve", lambda e: e.reciprocal(out=st[:, 4:5], in_=st[:, 3:4]), reads=[st], writes=[st])
        p.ts((pb_, pb_[:, 0:nk]), (pb_, pb_[:, 0:nk]), (st, st[:, 4:5]), None, ALU.mult, eng="pool")
        nt = (nk + 127) // 128
        for k4 in range(0, nt, 4):
            n4 = min(4, nt - k4)
            pt = PS()
            for i in range(n4):
                kw_ = min(128, nk - (k4 + i) * 128)
                p.mm((pt, pt[0:kw_, i * 128:(i + 1) * 128]), (pb_, pb_[:, (k4 + i) * 128:(k4 + i) * 128 + kw_]),
                     (identb, identb[:, :]))
            t4 = pT4[npt[0] % 2]
            npt[0] += 1
            if nk - k4 * 128 >= n4 * 128:
                p.cp((t4, t4[:, 0:n4 * 128]), (pt, pt[:, 0:n4 * 128]), eng="act")
            else:
                for i in range(n4):
                    kw_ = min(128, nk - (k4 + i) * 128)
                    p.cp((t4, t4[0:kw_, i * 128:(i + 1) * 128]), (pt, pt[0:kw_, i * 128:(i + 1) * 128]), eng="act")
            for i in range(n4):
                kt = k4 + i
                kw_ = min(128, nk - kt * 128)
                p.mm((po, po[:, 0:128]), (t4, t4[0:kw_, i * 128:(i + 1) * 128]),
                     (Vt, Vt[0:kw_, voff + kt * 128: voff + (kt + 1) * 128]), start=(kt == 0), stop=(kt == nt - 1))
            yield k4, n4, t4

    for g in range(G):
        for hh in range(4):
            r0 = rowB("nq", 0) + (g * 4 + hh) * 128
            p.dma(qT[:, hh * S:(hh + 1) * S], projB[r0:r0 + 128, :], writes=[qT])
        for n in ("kc", "vc", "ksl", "kw"):
            r0 = rowB(n, g)
            p.dma(KT[n][:, :], projB[r0:r0 + 128, :], writes=[KT[n]])
        for n in ("vsl", "vw"):
            r0 = rowB(n, g)
            p.dma(VT[:, :], projB[r0:r0 + 128, :], writes=[VT])
            for kt in range(0, NQ, 4):
                n4 = min(4, NQ - kt)
                pt = PS()
                for i in range(n4):
                    p.mm((pt, pt[:, i * 128:(i + 1) * 128]), (VT, VT[:, (kt + i) * 128:(kt + i + 1) * 128]), (identb, identb[:, :]))
                p.cp((Vtok[n], Vtok[n][:, kt * 128:(kt + n4) * 128]), (pt, pt[:, 0:n4 * 128]), eng="act")
        for kv, n in enumerate(("kc", "vc")):
            p.dma(w1f[:, :].rearrange("d (r e) -> d r e", e=128), w1_d[kv].rearrange("r d e -> d r e"), writes=[w1f])
            p.cp((w1b, w1b[:, :]), (w1f, w1f[:, :]))
            p.dma(w2f[:, :], w2_d[kv], writes=[w2f]); p.cp((w2b, w2b[:, :]), (w2f, w2f[:, :]))
            p.dma(posf[:, :], pos_d[kv], writes=[posf]); p.cp((posb, posb[:, :]), (posf, posf[:, :]))
            pc = PS()
            for r in range(32):
                p.mm((pc, pc[:, 0:1]), (w1b, w1b[:, r * 128:(r + 1) * 128]), (posb, posb[:, r:r + 1]), start=(r == 0), stop=(r == 31))
            p.cp((cvec, cvec[:, :]), (pc, pc[:, 0:1]))
            p.memset((hidT, hidT[:, :]), 0.0)
            for j0 in range(0, NCMP, 512):
                jw = min(512, NCMP - j0)
                ph = PS()
                for r in range(32):
                    lo = 16 * j0 + r
                    p.mm((ph, ph[:, 0:jw]), (w1b, w1b[:, r * 128:(r + 1) * 128]),
                         (KT[n], KT[n][:, lo: lo + 16 * (jw - 1) + 1: 16]), start=(r == 0), stop=(r == 31))
                p.act((hidT, hidT[:, j0:j0 + jw]), (ph, ph[:, 0:jw]), AF.Silu, bias=(cvec, cvec[:, 0:1]))
            if kv == 0:
                for j0 in range(0, NJ, 512):
                    jw = min(512, NJ - j0)
                    pk = PS()
                    p.mm((pk, pk[:, 0:jw]), (w2b, w2b[:, :]), (hidT, hidT[:, j0:j0 + jw]))
                    p.cp((kcmpT, kcmpT[:, j0:j0 + jw]), (pk, pk[:, 0:jw]), eng="act")
            else:
                for jt in range(NJT):
                    pv = PS()
                    p.mm((pv, pv[:, 0:128]), (hidT, hidT[:, jt * 128:(jt + 1) * 128]), (w2b, w2b[:, :]))
                    p.cp((vcmp, vcmp[:, jt * 128:(jt + 1) * 128]), (pv, pv[:, 0:128]), eng="act")
        for qi in range(NQ):
            t0 = qi * 128
            p.dma(cd[:, :], ncst["cmpd"][qi], writes=[cd]); p.dma(cm[:, :], ncst["cmpm"][qi], writes=[cm])
            p.dma(alw[:, :], ncst["allow"][qi], writes=[alw]); p.dma(fbt[:, :], ncst["fb"][qi], writes=[fbt])
            p.dma(gat[:, :], small[t0:t0 + 128, 2 * H:2 * H + 3 * NH], writes=[gat])
            p.act((gat, gat[:, :]), (gat, gat[:, :]), AF.Sigmoid)
            njv = min(NJ, ((t0 + 96) // 16 + 1 + 127) // 128 * 128)
            pimp = pss[0]
            for hh in range(4):
                hg = g * 4 + hh
                q_ap = (qT, qT[:, hh * S + t0: hh * S + t0 + 128])
                for j0 in range(0, njv, 512):
                    jw = min(512, njv - j0)
                    pscr = PS()
                    p.mm((pscr, pscr[:, 0:jw]), q_ap, (kcmpT, kcmpT[:, j0:j0 + jw]))
                    p.stt((sc, sc[:, j0:j0 + jw]), (cd, cd[:, j0:j0 + jw]), -slopes[hg], (pscr, pscr[:, 0:jw]), ALU.mult, ALU.add)
                    p.tt((sc, sc[:, j0:j0 + jw]), (sc, sc[:, j0:j0 + jw]), (cm, cm[:, j0:j0 + jw]), ALU.add, eng="pool")
                po = PO()
                for k4, n4, t4 in softmax_pv(njv, vcmp, 0, po):
                    for i in range(n4):
                        jt = k4 + i
                        first = (hh == 0 and jt == 0)
                        last = (hh == 3 and jt == njv // 128 - 1)
                        p.mm((pimp, pimp[:, 0:NS]), (t4, t4[:, i * 128:(i + 1) * 128]), (c2s, c2s[:, jt * NS:(jt + 1) * NS]),
                             start=first, stop=last)
                obt = obh[hh]
                p.ts((obt, obt[:, :]), (po, po[:, 0:128]), (gat, gat[:, 0 * NH + hg:0 * NH + hg + 1]), None, ALU.mult)
            nkeys = t0 + 128
            p.tt((score, score[:, :]), (pimp, pimp[:, 0:NS]), (alw, alw[:, :]), ALU.mult)
            p.tt((score, score[:, :]), (score, score[:, :]), (fbt, fbt[:, :]), ALU.add)
            if NS > 16:
                p.op("dve", lambda e: e.max(out=m8[:, 0:8], in_=score[:, :]), reads=[score], writes=[m8])
                p.op("dve", lambda e: e.match_replace(out=sc2[:, :], in_to_replace=m8[:, 0:8], in_values=score[:, :],
                                                      imm_value=-2e30), reads=[m8, score], writes=[sc2])
                p.op("dve", lambda e: e.max(out=m8[:, 8:16], in_=sc2[:, :]), reads=[sc2], writes=[m8])
                p.ts((m8, m8[:, 0:1]), (m8, m8[:, 15:16]), -1e29, None, ALU.max)
                p.ts((selm, selm[:, :]), (score, score[:, :]), (m8, m8[:, 0:1]), None, ALU.is_ge)
            else:
                p.ts((selm, selm[:, :]), (score, score[:, :]), -1e29, None, ALU.is_ge)
            p.ts((selm, selm[:, :]), (selm, selm[:, :]), -1.0, 1e30, ALU.add, ALU.mult)
            nb = nkeys // 64
            p.op("dve", lambda e, nb=nb: e.tensor_copy(
                out=selb[:, 0:nb * 64].rearrange("p (n k) -> p n k", k=64),
                in_=bass.AP(selm.t, 0, [[NS, 128], [1, nb], [0, 64]])), reads=[selm], writes=[selb])
            p.tt((selb, selb[:, t0:t0 + 128]), (selb, selb[:, t0:t0 + 128]), (caus, caus[:, :]), ALU.add)
            for hh in range(4):
                hg = g * 4 + hh
                q_ap = (qT, qT[:, hh * S + t0: hh * S + t0 + 128])
                obt = obh[hh]
                for br, (kn, vn) in ((1, ("ksl", "vsl")), (2, ("kw", "vw"))):
                    klo = 0 if br == 1 else max(0, t0 - 512)
                    nk = nkeys - klo
                    for kb in range(0, nk, 512):
                        bw = min(512, nk - kb)
                        pscr = PS()
                        p.mm((pscr, pscr[:, 0:bw]), q_ap, (KT[kn], KT[kn][:, klo + kb: klo + kb + bw]))
                        c0 = klo + kb - t0 + S
                        p.stt((sc, sc[:, kb:kb + bw]), (relc, relc[:, c0:c0 + bw]), -slopes[hg], (pscr, pscr[:, 0:bw]), ALU.mult, ALU.add)
                        if br == 1:
                            p.tt((sc, sc[:, kb:kb + bw]), (sc, sc[:, kb:kb + bw]), (selb, selb[:, kb:kb + bw]), ALU.add, eng="pool")
                        else:
                            b0 = klo + kb - (t0 - 512)
                            p.tt((sc, sc[:, kb:kb + bw]), (sc, sc[:, kb:kb + bw]), (band, band[:, b0:b0 + bw]), ALU.add, eng="pool")
                    po = PO()
                    for _ in softmax_pv(nk, Vtok[vn], klo, po):
                        pass
                    p.stt((obt, obt[:, :]), (po, po[:, 0:128]), (gat, gat[:, br * NH + hg: br * NH + hg + 1]), (obt, obt[:, :]), ALU.mult, ALU.add)
                ptr = PS()
                p.tr((ptr, ptr[:, 0:128]), (obt, obt[:, :]), (ident, ident[:, :]))
                p.cp((obst, obst[:, hh * S + t0: hh * S + t0 + 128]), (ptr, ptr[:, 0:128]), eng="act")
        for hh in range(4):
            r0 = (g * 4 + hh) * 128
            p.dma(obT[r0:r0 + 128, :], obst[:, hh * S:(hh + 1) * S], reads=[obst], q="pool")
    p.pop()


class LinCtx:
    def __init__(self, p, nacc=4, CT=256):
        self.p = p
        self.CT = CT
        self.wf = [p.sb(f"lwf{i}", [128, 32 * CT]) for i in range(2)]
        self.wb = [p.sb(f"lwb{i}", [128, 32 * CT], BF16) for i in range(2)]
        self.acc = [p.ps(f"lacc{i}", [128, 512]) for i in range(nacc)]
        self.nw = 0
        self.na = 0


def tile_w(W, CT):
    K, N = W.shape
    assert N % CT == 0 and K % 128 == 0
    return np.ascontiguousarray(W.reshape(K // 128, 128, N // CT, CT).transpose(2, 1, 0, 3))


def linear_fm(L, Wt, nk, act_fn, tokw, consume, ct0=0, nct=None, k_off=0, kgrp=32):
    p = L.p
    CT = L.CT
    assert Wt.shape[3] == CT
    if nct is None:
        nct = Wt.shape[0] - ct0
    for ct in range(ct0, ct0 + nct):
        c0 = ct * CT
        cw = CT
        nblk = (cw + 127) // 128
        accs = []
        for b in range(nblk):
            accs.append(L.acc[L.na % len(L.acc)])
            L.na += 1
        for k0 in range(0, nk, kgrp):
            kg = min(kgrp, nk - k0)
            wfb = L.wf[L.nw % 2]
            wbb = L.wb[L.nw % 2]
            L.nw += 1
            p.dma(wfb[:, 0:kg * cw].rearrange("p (k n) -> p k n", n=cw), Wt[ct, :, k_off + k0:k_off + k0 + kg, :], writes=[wfb])
            half = (kg * cw) // 2
            p.cp((wbb, wbb[:, 0:half]), (wfb, wfb[:, 0:half]), eng="dve")
            p.cp((wbb, wbb[:, half:kg * cw]), (wfb, wfb[:, half:kg * cw]), eng="pool")
            for b in range(nblk):
                bw = min(128, cw - b * 128)
                for k in range(kg):
                    p.mm((accs[b], accs[b][0:bw, 0:tokw]), (wbb, wbb[:, k * cw + b * 128: k * cw + b * 128 + bw]),
                         act_fn(k0 + k), start=(k0 + k == 0), stop=(k0 + k == nk - 1))
        for b in range(nblk):
            consume(c0 + b * 128, accs[b])


def layernorm_fm(p, C, rT, TT, lng, lnb, outT_d, tt0, cst, tmp, stat_ps, mean_t, rstd_t, ost):
    KC, D = C.KC, C.D
    ones = cst["ones"]
    s1, s2 = stat_ps
    for kc in range(KC):
        p.mm((s1, s1[:, 0:TT]), (ones, ones[:, :]), (rT, rT[:, kc * TT:(kc + 1) * TT]), start=(kc == 0), stop=(kc == KC - 1))
    for kc in range(KC):
        t = tmp[kc % 2]
        p.act((t, t[:, 0:TT]), (rT, rT[:, kc * TT:(kc + 1) * TT]), AF.Square)
        p.mm((s2, s2[:, 0:TT]), (ones, ones[:, :]), (t, t[:, 0:TT]), start=(kc == 0), stop=(kc == KC - 1))
    p.act((mean_t, mean_t[:, 0:TT]), (s1, s1[:, 0:TT]), AF.Copy, scale=1.0 / D)
    t = tmp[0]
    p.tt((t, t[:, 0:TT]), (mean_t, mean_t[:, 0:TT]), (mean_t, mean_t[:, 0:TT]), ALU.mult)
    p.stt((rstd_t, rstd_t[:, 0:TT]), (s2, s2[:, 0:TT]), 1.0 / D, (t, t[:, 0:TT]), ALU.mult, ALU.subtract)
    p.rsqrt((rstd_t, rstd_t[:, 0:TT]), (rstd_t, rstd_t[:, 0:TT]), 1e-5, 1.0)
    for kc in range(KC):
        o = ost[kc % 2]
        e = "dve" if kc % 2 == 0 else "pool"
        p.tt((o, o[:, 0:TT]), (rT, rT[:, kc * TT:(kc + 1) * TT]), (mean_t, mean_t[:, 0:TT]), ALU.subtract, eng=e)
        p.tt((o, o[:, 0:TT]), (o, o[:, 0:TT]), (rstd_t, rstd_t[:, 0:TT]), ALU.mult, eng=e)
        p.ts((o, o[:, 0:TT]), (o, o[:, 0:TT]), (lng, lng[:, kc:kc + 1]), (lnb, lnb[:, kc:kc + 1]), ALU.mult, ALU.add, eng=e)
        p.dma(outT_d[kc * 128:(kc + 1) * 128, tt0:tt0 + TT], o[:, 0:TT], reads=[o], q="pool")


def stage_post1(p, C, xT_d, oaT, obT, projF, wra, wrb, wout, g1p, lng, lnb, x1T_d, cst, alpha):
    S, KC, H, NH, D = C.S, C.KC, C.H, C.NH, C.D
    TT = min(512, S)
    p.push()
    L = LinCtx(p, CT=128)
    oa = p.sb("oa", [128, H * TT], BF16); obt = p.sb("obt", [128, NH * TT], BF16)
    yT = p.sb("yT", [128, KC * TT], BF16)
    rT = p.sb("rT", [128, KC * TT])
    sg = [p.sb(f"sg{i}", [128, TT]) for i in range(2)]
    ya = p.sb("ya", [128, TT])
    xb = [p.sb(f"xb{i}", [128, TT]) for i in range(2)]
    tmp = [p.sb(f"ltmp{i}", [128, TT]) for i in range(2)]
    ost = [p.sb(f"lost{i}", [128, TT]) for i in range(2)]
    mean_t = p.sb("mean_t", [128, TT]); rstd_t = p.sb("rstd_t", [128, TT])
    stat_ps = (p.ps("s1", [128, 512]), p.ps("s2", [128, 512]))
    for tt0 in range(0, S, TT):
        p.dma(oa[:, :].rearrange("p (k t) -> p k t", t=TT), oaT.rearrange("(k p) t -> p k t", p=128)[:, :, tt0:tt0 + TT], writes=[oa])
        p.dma(obt[:, :].rearrange("p (k t) -> p k t", t=TT), obT.rearrange("(k p) t -> p k t", p=128)[:, :, tt0:tt0 + TT], writes=[obt])

        def cons_a(c, ps):
            s = sg[0]
            r0 = C.off["ma"] + c
            p.dma(s[:, 0:TT], projF[r0:r0 + 128, tt0:tt0 + TT], writes=[s])
            p.act((s, s[:, 0:TT]), (s, s[:, 0:TT]), AF.Sigmoid)
            kc = c // 128
            p.tt((yT, yT[:, kc * TT:(kc + 1) * TT]), (ps, ps[:, 0:TT]), (s, s[:, 0:TT]), ALU.mult)

        def cons_b(c, ps):
            s = sg[1]
            r0 = C.off["mb"] + c
            p.dma(s[:, 0:TT], projF[r0:r0 + 128, tt0:tt0 + TT], writes=[s])
            p.act((s, s[:, 0:TT]), (s, s[:, 0:TT]), AF.Sigmoid)
            kc = c // 128
            p.tt((ya, ya[:, 0:TT]), (ps, ps[:, 0:TT]), (s, s[:, 0:TT]), ALU.mult)
            p.tt((yT, yT[:, kc * TT:(kc + 1) * TT]), (yT, yT[:, kc * TT:(kc + 1) * TT]), (ya, ya[:, 0:TT]), ALU.add, eng="pool")

        linear_fm(L, wra, H, lambda k: (oa, oa[:, k * TT:(k + 1) * TT]), TT, cons_a)
        linear_fm(L, wrb, NH, lambda k: (obt, obt[:, k * TT:(k + 1) * TT]), TT, cons_b)

        def cons_o(c, ps):
            kc = c // 128
            x = xb[kc % 2]
            p.dma(x[:, 0:TT], xT_d[c:c + 128, tt0:tt0 + TT], writes=[x])
            t = tmp[kc % 2]
            p.act((t, t[:, 0:TT]), (ps, ps[:, 0:TT]), AF.Copy, scale=(g1p, g1p[:, kc:kc + 1]))
            p.stt((rT, rT[:, kc * TT:(kc + 1) * TT]), (x, x[:, 0:TT]), alpha, (t, t[:, 0:TT]), ALU.mult, ALU.add)

        linear_fm(L, wout, KC, lambda k: (yT, yT[:, k * TT:(k + 1) * TT]), TT, cons_o)
        layernorm_fm(p, C, rT, TT, lng, lnb, x1T_d, tt0, cst, tmp, stat_ps, mean_t, rstd_t, ost)
    p.pop()


def stage_post2(p, C, x1T_d, w1, b1, w2, b2, sc2p, sh2, g2p, lng, lnb, x2T_d, cst, alpha):
    S, KC, D, DFF = C.S, C.KC, C.D, C.DFF
    FC = DFF // 128
    FG = min(32, FC)
    TT = min(512, S)
    p.push()
    L = LinCtx(p, CT=128)
    uT = p.sb("uT2", [128, KC * TT], BF16)
    xk = p.sb("xk", [128, KC * TT])
    h1T = p.sb("h1T", [128, FG * TT], BF16)
    gb = p.sb("gb", [128, KC])
    tmp = [p.sb(f"mtmp{i}", [128, TT]) for i in range(2)]
    ost = [p.sb(f"most{i}", [128, TT]) for i in range(2)]
    mean_t = p.sb("mean_t2", [128, TT]); rstd_t = p.sb("rstd_t2", [128, TT])
    stat_ps = (p.ps("s1b", [128, 512]), p.ps("s2b", [128, 512]))
    p.tt((gb, gb[:, :]), (g2p, g2p[:, :]), (b2, b2[:, :]), ALU.mult)
    for tt0 in range(0, S, TT):
        p.dma(xk[:, :].rearrange("p (k t) -> p k t", t=TT), x1T_d.rearrange("(k p) t -> p k t", p=128)[:, :, tt0:tt0 + TT], writes=[xk])
        for kc in range(KC):
            sl = slice(kc * TT, (kc + 1) * TT)
            p.ts((uT, uT[:, sl]), (xk, xk[:, sl]), (sc2p, sc2p[:, kc:kc + 1]), (sh2, sh2[:, kc:kc + 1]), ALU.mult, ALU.add)
            p.ts((xk, xk[:, sl]), (xk, xk[:, sl]), alpha, (gb, gb[:, kc:kc + 1]), ALU.mult, ALU.add)
        for f0 in range(0, FC, FG):
            def cons_h(c, ps, f0=f0):
                fc = c // 128
                t = tmp[fc % 2]
                p.act((t, t[:, 0:TT]), (ps, ps[:, 0:TT]), AF.Relu, bias=(b1, b1[:, fc:fc + 1]))
                fl = fc - f0
                p.tt((h1T, h1T[:, fl * TT:(fl + 1) * TT]), (t, t[:, 0:TT]), (t, t[:, 0:TT]), ALU.mult, eng=("dve" if fc % 2 else "pool"))

            linear_fm(L, w1, KC, lambda k: (uT, uT[:, k * TT:(k + 1) * TT]), TT, cons_h, ct0=f0, nct=FG)

            def cons_o(c, ps):
                kc = c // 128
                sl = slice(kc * TT, (kc + 1) * TT)
                p.stt((xk, xk[:, sl]), (ps, ps[:, 0:TT]), (g2p, g2p[:, kc:kc + 1]), (xk, xk[:, sl]), ALU.mult, ALU.add)

            linear_fm(L, w2, FG, lambda k: (h1T, h1T[:, k * TT:(k + 1) * TT]), TT, cons_o, k_off=f0)
        layernorm_fm(p, C, xk, TT, lng, lnb, x2T_d, tt0, cst, tmp, stat_ps, mean_t, rstd_t, ost)
    p.pop()


PV_NAMES = ["sc1", "sh1", "g1", "sc2", "sh2", "g2"]


def build_seq(C, ncn_shapes):
    nc = bass.Bass("TRN2", target_bir_lowering=False)
    L, D, S, KC, H, NH = C.L, C.D, C.S, C.KC, C.H, C.NH
    di = lambda n, sh, dt=F32: nc.dram_tensor(n, list(sh), dt, kind="ExternalInput").ap()
    xT0 = di("xT", [D, S])
    cT = di("cT", [128, KC])
    ada_w = di("ada_w", [6 * D // 256, 128, KC, 256])
    adabT = di("adabT", [128, 6 * KC])
    tabT = di("tabT", [L, 128, 6 * KC])
    w_in = di("w_in", [L, C.NBIG // 256, 128, KC, 256])
    w_sm = di("w_sm", [L, 128, KC, C.NSM])
    convw = di("convw", [L, 128, 3 * H * 4]); alog = di("alog", [L, 128, H]); dtb = di("dtb", [L, 128, H])
    normw = di("normw", [L, 128, 1])
    cpos = di("cpos", [L, 2, 128, 32]); cw1 = di("cw1", [L, 2, 32, 128, 128]); cw2 = di("cw2", [L, 2, 128, 128])
    wra = di("wra", [L, KC, 128, H, 128]); wrb = di("wrb", [L, KC, 128, NH, 128]); wout = di("wout", [L, KC, 128, KC, 128])
    w1 = di("w1", [L, C.DFF // 128, 128, KC, 128]); w2 = di("w2", [L, KC, 128, C.DFF // 128, 128])
    vecs = di("vecs", [L, 128, 5 * KC + C.DFF // 128])
    ncst = {k: di("nc_" + k, v) for k, v in ncn_shapes.items()}
    outT = nc.dram_tensor("outT", [D, S], F32, kind="ExternalOutput").ap()
    ds = lambda n, sh, dt=F32: nc.dram_tensor(n, list(sh), dt, kind="Internal").ap()
    projF = ds("projF", [C.NF, S]); projB = ds("projB", [C.NB, S], BF16); small = ds("small", [S, C.NSM])
    oaT = ds("oaT", [H * 128, S], BF16); obT = ds("obT", [NH * 128, S], BF16)
    x1T = ds("x1T", [D, S]); xmid = ds("xmid", [D, S])
    alpha = (2.0 * L) ** 0.25
    p = Prog(nc)
    cst = load_consts(p, nc, C)
    md = p.sb("md", [128, 6 * KC])
    p.push()
    L0 = LinCtx(p)
    csf = p.sb("csf", [128, KC]); p.dma(csf[:, :], cT, writes=[csf])
    csb = p.sb("csb", [128, KC], BF16)
    p.act((csb, csb[:, :]), (csf, csf[:, :]), AF.Silu)
    adab = p.sb("adab", [128, 6 * KC]); p.dma(adab[:, :], adabT, writes=[adab])

    def cons_mod(c, ps):
        j = c // 128
        p.tt((md, md[:, j:j + 1]), (ps, ps[:, 0:1]), (adab, adab[:, j:j + 1]), ALU.add)

    linear_fm(L0, ada_w, KC, lambda k: (csb, csb[:, k:k + 1]), 1, cons_mod)
    p.pop()
    xin = xT0
    for l in range(L):
        p.push()
        mt = p.sb("mt", [128, 6 * KC]); p.dma(mt[:, :], tabT[l], writes=[mt])
        p.tt((mt, mt[:, :]), (mt, mt[:, :]), (md, md[:, :]), ALU.add)
        for j in (1, 2, 4, 5):
            p.ts((mt, mt[:, j * KC:(j + 1) * KC]), (mt, mt[:, j * KC:(j + 1) * KC]), 1.0, None, ALU.add)
        V = lambda j: View(mt, mt[:, j * KC:(j + 1) * KC])
        sh1, sc1p, g1p, sh2, sc2p, g2p = (V(j) for j in range(6))
        vt = p.sb("vt", [128, 5 * KC + C.DFF // 128]); p.dma(vt[:, :], vecs[l], writes=[vt])
        W = lambda j: View(vt, vt[:, j * KC:(j + 1) * KC])
        ln1g, ln1b, b2v, ln2g, ln2b = (W(j) for j in range(5))
        b1v = View(vt, vt[:, 5 * KC:5 * KC + C.DFF // 128])
        xout = outT if l == L - 1 else xmid
        stage_proj(p, C, xin, w_in[l], w_sm[l], (sc1p, sh1), projF, projB, small, cst)
        stage_gdn(p, C, projF, small, (convw[l], alog[l], dtb[l], normw[l]), oaT, cst)
        stage_nsa(p, C, nc, projB, small, (cpos[l], cw1[l], cw2[l]), ncst, obT, cst)
        stage_post1(p, C, xin, oaT, obT, projF, wra[l], wrb[l], wout[l], g1p, ln1g, ln1b, x1T, cst, alpha)
        stage_post2(p, C, x1T, w1[l], b1v, w2[l], b2v, sc2p, sh2, g2p, ln2g, ln2b, xout, cst, alpha)
        p.pop()
        xin = xmid
    p.finish()
    return nc, p


def fm(v, kc):
    return np.ascontiguousarray(v.reshape(kc, 128).T)


def seq_inputs(C, b, x, inp):
    L, KC, H = C.L, C.KC, C.H
    perm = win_perm(C)
    m = {}
    m["xT"] = np.ascontiguousarray(x[b].T)
    m["cT"] = fm(inp["c"][b], KC)
    m["ada_w"] = tile_w(inp["ada_w"], 256)
    m["adabT"] = fm(inp["ada_b"], 6 * KC)
    m["tabT"] = np.stack([np.concatenate([fm(inp["ada_table"][l, j], KC) for j in range(6)], axis=1) for l in range(L)])
    wp = [inp["w_in"][l][:, perm] for l in range(L)]
    m["w_in"] = np.stack([tile_w(w[:, :C.NBIG], 256) for w in wp])
    m["w_sm"] = np.stack([np.ascontiguousarray(w[:, C.NBIG:].reshape(KC, 128, C.NSM).transpose(1, 0, 2)) for w in wp])
    m["convw"] = np.stack([inp["gdn_conv_w"][l].reshape(4, 3 * H, 128).transpose(2, 1, 0).reshape(128, 3 * H * 4) for l in range(L)])
    m["alog"] = np.ascontiguousarray(np.broadcast_to(inp["gdn_a_log"][:, None, :], (L, 128, H)))
    m["dtb"] = np.ascontiguousarray(np.broadcast_to(inp["gdn_dt_bias"][:, None, :], (L, 128, H)))
    m["normw"] = np.ascontiguousarray(inp["gdn_norm_w"][:, :, None])
    m["cpos"] = np.ascontiguousarray(inp["cmp_pos"].transpose(0, 1, 3, 2))
    m["cw1"] = inp["cmp_w1"]; m["cw2"] = inp["cmp_w2"]
    for k, src in (("wra", "w_read_a"), ("wrb", "w_read_b"), ("wout", "w_out"), ("w1", "mlp_w1"), ("w2", "mlp_w2")):
        m[k] = np.stack([tile_w(inp[src][l], 128) for l in range(L)])
    m["vecs"] = np.stack([np.concatenate([fm(inp[k][l], KC) for k in ("ln1_g", "ln1_b", "mlp_b2", "ln2_g", "ln2_b")]
                                         + [fm(inp["mlp_b1"][l], C.DFF // 128)], axis=1) for l in range(L)])
    m["cst_f32"] = consts_np()
    for k, v in nsa_consts_np(C).items():
        m["nc_" + k] = v
    return {k: np.ascontiguousarray(v, dtype=np.float32) for k, v in m.items()}


def kernel(**inp):
    inp = {k: np.asarray(v) for k, v in inp.items()}
    x = inp["x"]
    B = x.shape[0]
    C = Cfg()
    ncn = nsa_consts_np(C)
    nc, _ = build_seq(C, {k: v.shape for k, v in ncn.items()})
    base = seq_inputs(C, 0, x, inp)
    in_maps = [base]
    for b in range(1, B):
        m = dict(base)
        m["xT"] = np.ascontiguousarray(x[b].T, dtype=np.float32)
        m["cT"] = np.ascontiguousarray(fm(inp["c"][b], C.KC), dtype=np.float32)
        in_maps.append(m)
    res = run_bass_kernel_spmd(nc, in_maps, core_ids=list(range(B))).results
    return np.stack([np.ascontiguousarray(r["outT"].T) for r in res]).astype(np.float32)
```

```python
import numpy as np
from contextlib import ExitStack
import concourse.bass as bass
import concourse.mybir as mybir
from concourse.bass_utils import run_bass_kernel_spmd

F32 = mybir.dt.float32
BF16 = mybir.dt.bfloat16
AF = mybir.ActivationFunctionType
ALU = mybir.AluOpType
AX = mybir.AxisListType

NCORES = 8


class Buf:
    def __init__(self, prog, name, t, space):
        self.prog = prog
        self.name = name
        self.t = t
        self.space = space
        self.last_w = None
        self.readers = []
        self.dsem = None
        self.dcount = 0

    def __getitem__(self, idx):
        return self.t[idx]


class Prog:
    ENGS = ("pe", "act", "dve", "pool", "sp")

    def __init__(self, nc):
        self.nc = nc
        self.es = ExitStack()
        self.scopes = [self.es]
        self.lists = {e: [] for e in self.ENGS}
        self.sems = {}
        self.counts = {e: 0 for e in self.ENGS}
        for e in ("pe", "act", "dve", "pool"):
            self.sems[e] = self.es.enter_context(nc.semaphore("s_" + e))
        self.waited = {e: {} for e in self.ENGS}
        self.dma_bufs = []
        self.uid = 0
        self.free_dsems = {"sp": [], "pool": [], "act": []}
        self.scope_bufs = [[]]

    def push(self):
        s = ExitStack()
        self.scopes.append(s)
        self.scope_bufs.append([])

    def pop(self):
        self.barrier()
        for b in self.scope_bufs.pop():
            self.free_dsems[b.dq].append((b.dsem, b.dcount))
        self.scopes.pop().close()

    def sb(self, name, shape, dt=F32):
        self.uid += 1
        t = self.scopes[-1].enter_context(self.nc.sbuf_tensor(f"{name}_{self.uid}", list(shape), dt))
        return Buf(self, f"{name}_{self.uid}", t, "sb")

    def ps(self, name, shape, dt=F32):
        self.uid += 1
        t = self.scopes[-1].enter_context(self.nc.psum_tensor(f"{name}_{self.uid}", list(shape), dt))
        return Buf(self, f"{name}_{self.uid}", t, "ps")

    def _need(self, eng, dep, lst):
        if dep is None:
            return
        if dep[0] == "dma":
            _, buf, cnt = dep
            key = ("dma", id(buf))
            sem, val = buf.dsem, cnt
        else:
            e, cnt = dep
            if e == eng and e == "pe":
                return
            key = e
            sem, val = self.sems[e], cnt
        if self.waited[eng].get(key, 0) >= val:
            return
        self.waited[eng][key] = val
        lst.append((sem, val))

    def op(self, eng, fn, reads=(), writes=()):
        waits = []
        reads = [getattr(b, "buf", b) for b in reads]
        writes = [getattr(b, "buf", b) for b in writes]
        for b in reads:
            self._need(eng, b.last_w, waits)
        for b in writes:
            self._need(eng, b.last_w, waits)
            for r in b.readers:
                self._need(eng, r, waits)
        self.counts[eng] += 1
        me = (eng, self.counts[eng])
        for b in reads:
            b.readers.append(me)
            if len(b.readers) > 40:
                last = {}
                for r in b.readers:
                    k = r[0] if r[0] != "dma" else ("dma", id(r[1]))
                    last[k] = r
                b.readers = list(last.values())
        for b in writes:
            b.last_w = me
            b.readers = []
        self.lists[eng].append((waits, fn, (self.sems[eng], 1)))

    def dma(self, out_ap, in_ap, reads=(), writes=(), q="sp", **kw):
        waits = []
        reads = [getattr(b, "buf", b) for b in reads]
        writes = [getattr(b, "buf", b) for b in writes]
        bufs = list(reads) + list(writes)
        assert len(bufs) == 1
        b = bufs[0]
        self._need(q, b.last_w, waits)
        if writes:
            for r in b.readers:
                self._need(q, r, waits)
        if b.dsem is None:
            b.dq = q
            if self.free_dsems[q]:
                b.dsem, b.dcount = self.free_dsems[q].pop()
            else:
                b.dsem = self.es.enter_context(self.nc.semaphore("d_" + b.name))
            self.scope_bufs[-1].append(b)
        assert b.dq == q, (b.name, b.dq, q)
        if not any(x is b for x in self.dma_bufs):
            self.dma_bufs.append(b)
        b.dcount += 16
        me = ("dma", b, b.dcount)
        if reads:
            b.readers.append(me)
        else:
            b.last_w = me
            b.readers = []
        fn = lambda e, o=out_ap, i=in_ap, kw=kw: e.dma_start(out=o, in_=i, **kw)
        self.lists[q].append((waits, fn, (b.dsem, 16)))

    def barrier(self):
        for e in self.ENGS:
            waits = []
            for o in ("pe", "act", "dve", "pool"):
                if o != e and self.counts[o] > 0:
                    self._need(e, (o, self.counts[o]), waits)
            for b in self.dma_bufs:
                self._need(e, ("dma", b, b.dcount), waits)
            if waits:
                self.lists[e].append((waits, None, None))
        self.dma_bufs = []

    def finish(self):
        self.barrier()
        nc = self.nc
        engmap = {"pe": "tensor", "act": "scalar", "dve": "vector", "pool": "gpsimd", "sp": "sync"}
        with nc.Block() as block:
            for ename, attr in engmap.items():
                lst = self.lists[ename]

                def body(e, lst=lst):
                    for waits, fn, inc in lst:
                        for sem, val in waits:
                            e.wait_ge(sem, val)
                        if fn is not None:
                            fn(e).then_inc(inc[0], inc[1])

                getattr(block, attr)(body)
        while self.scopes:
            self.scopes.pop().close()

    def mm(self, out, lhsT, rhs, start=True, stop=True):
        self.op("pe", lambda e: e.matmul(out[1], lhsT[1], rhs[1], start=start, stop=stop),
                reads=[lhsT[0], rhs[0]], writes=[out[0]])

    def tr(self, out, in_, ident):
        self.op("pe", lambda e: e.matmul(out[1], in_[1], ident[1], start=True, stop=True),
                reads=[in_[0], ident[0]], writes=[out[0]])

    def act(self, out, in_, func, bias=None, scale=None, accum=None, eng="act"):
        kw = {}
        rd = [in_[0]]
        wr = [out[0]]
        if bias is not None:
            if isinstance(bias, tuple):
                kw["bias"] = bias[1]; rd.append(bias[0])
            else:
                kw["bias"] = bias
        if scale is not None:
            if isinstance(scale, tuple):
                kw["scale"] = scale[1]; rd.append(scale[0])
            else:
                kw["scale"] = scale
        if accum is not None:
            kw["accum_out"] = accum[1]; wr.append(accum[0])
        self.op("act", lambda e: e.activation(out=out[1], in_=in_[1], func=func, **kw), reads=rd, writes=wr)

    def tt(self, out, in0, in1, op, eng="dve"):
        self.op(eng, lambda e: e.tensor_tensor(out=out[1], in0=in0[1], in1=in1[1], op=op),
                reads=[in0[0], in1[0]], writes=[out[0]])

    def ts(self, out, in0, s1, s2, op0, op1=None, eng="dve", accum=None):
        rd = [in0[0]]
        wr = [out[0]]
        a1 = s1
        if isinstance(s1, tuple):
            a1 = s1[1]; rd.append(s1[0])
        a2 = s2
        if isinstance(s2, tuple):
            a2 = s2[1]; rd.append(s2[0])
        kw = {}
        if op1 is not None:
            kw["op1"] = op1
        if accum is not None:
            kw["accum_out"] = accum[1]; wr.append(accum[0])
        self.op(eng, lambda e: e.tensor_scalar(out=out[1], in0=in0[1], scalar1=a1, scalar2=a2, op0=op0, **kw),
                reads=rd, writes=wr)

    def stt(self, out, in0, s, in1, op0, op1, eng="dve"):
        rd = [in0[0], in1[0]]
        a = s
        if isinstance(s, tuple):
            a = s[1]; rd.append(s[0])
        self.op(eng, lambda e: e.scalar_tensor_tensor(out=out[1], in0=in0[1], scalar=a, in1=in1[1], op0=op0, op1=op1),
                reads=rd, writes=[out[0]])

    def cp(self, out, in_, eng="dve"):
        if eng == "act":
            self.op("act", lambda e: e.copy(out=out[1], in_=in_[1]), reads=[in_[0]], writes=[out[0]])
        else:
            self.op(eng, lambda e: e.tensor_copy(out=out[1], in_=in_[1]), reads=[in_[0]], writes=[out[0]])

    def rsqrt(self, out, in_, eps, scale):
        self.act(out, in_, AF.Sqrt, bias=eps, scale=scale)
        self.op("dve", lambda e: e.reciprocal(out=out[1], in_=out[1]), reads=[out[0]], writes=[out[0]])

    def memset(self, out, val, eng="dve"):
        self.op(eng, lambda e: e.memset(out[1], val), writes=[out[0]])


def _run(nc, in_maps):
    res = run_bass_kernel_spmd(nc, in_maps, core_ids=list(range(len(in_maps))))
    return res.results


D = 4096
KC = D // 128
NADA = 6 * D
MOD_COLS = NADA // NCORES


def build_mod():
    nc = bass.Bass("TRN2", target_bir_lowering=False)
    cT = nc.dram_tensor("cT", [128, KC * 4], F32, kind="ExternalInput").ap()
    w = nc.dram_tensor("w", [D, MOD_COLS], F32, kind="ExternalInput").ap()
    bias = nc.dram_tensor("b", [4, MOD_COLS], F32, kind="ExternalInput").ap()
    out = nc.dram_tensor("mod", [4, MOD_COLS], F32, kind="ExternalOutput").ap()
    p = Prog(nc)
    c_sb = p.sb("c_sb", [128, KC * 4])
    cs_sb = p.sb("cs_sb", [128, KC * 4])
    b_sb = p.sb("b_sb", [4, MOD_COLS])
    o_sb = p.sb("o_sb", [4, MOD_COLS])
    NW = 4
    wb = [p.sb(f"wb{i}", [128, 8, 512]) for i in range(NW)]
    pss = [p.ps(f"ps{i}", [128, 512]) for i in range(2)]
    p.dma(c_sb[:, :], cT, writes=[c_sb])
    p.dma(b_sb[:, :], bias, writes=[b_sb])
    p.op("act", lambda e: e.activation(out=cs_sb[:, :], in_=c_sb[:, :], func=AF.Silu),
         reads=[c_sb], writes=[cs_sb])
    wv = w.rearrange("(kc p) n -> p kc n", p=128)
    it = 0
    for ct in range(MOD_COLS // 512):
        ps = pss[ct % 2]
        for kg in range(KC // 8):
            wbuf = wb[it % NW]
            it += 1
            p.dma(wbuf[:, :, :], wv[:, kg * 8:(kg + 1) * 8, ct * 512:(ct + 1) * 512], writes=[wbuf])
            for k8 in range(8):
                kc = kg * 8 + k8
                p.op("pe", lambda e, ps=ps, kc=kc, wbuf=wbuf, k8=k8: e.matmul(
                    ps[0:4, :], cs_sb[:, kc * 4:(kc + 1) * 4], wbuf[:, k8, :],
                    start=(kc == 0), stop=(kc == KC - 1)),
                    reads=[cs_sb, wbuf], writes=[ps])
        p.op("dve", lambda e, ps=ps, ct=ct: e.tensor_tensor(
            out=o_sb[:, ct * 512:(ct + 1) * 512], in0=ps[0:4, :],
            in1=b_sb[:, ct * 512:(ct + 1) * 512], op=ALU.add),
            reads=[ps, b_sb], writes=[o_sb])
    p.dma(out, o_sb[:, :], reads=[o_sb])
    p.finish()
    return nc


def run_mod(c, ada_w, ada_b):
    cT = np.ascontiguousarray(c.T.reshape(KC, 128, 4).transpose(1, 0, 2).reshape(128, KC * 4))
    in_maps = []
    for i in range(NCORES):
        sl = slice(i * MOD_COLS, (i + 1) * MOD_COLS)
        in_maps.append({"cT": cT, "w": np.ascontiguousarray(ada_w[:, sl]),
                        "b": np.ascontiguousarray(np.broadcast_to(ada_b[sl], (4, MOD_COLS)))})
    res = _run(build_mod(), in_maps)
    return np.concatenate([r["mod"] for r in res], axis=1)


class Cfg:
    def __init__(self, D=4096, S=4096, H=16, G=4, DFF=16384, L=2):
        self.D, self.S, self.H, self.G, self.DFF, self.L = D, S, H, G, DFF, L
        self.KC = D // 128
        self.NH = 4 * G
        segs = [("gq", H * 128), ("gk", H * 128), ("gv", H * 128), ("gz", H * 128),
                ("ma", D), ("mb", D),
                ("nq", self.NH * 128), ("kc", G * 128), ("vc", G * 128), ("ksl", G * 128),
                ("vsl", G * 128), ("kw", G * 128), ("vw", G * 128)]
        self.off = {}
        o = 0
        for n, w in segs:
            self.off[n] = o
            o += w
        self.NF = self.off["nq"]
        self.NB = o - self.NF
        self.NBIG = o
        self.NSM = 2 * H + 3 * self.NH
        self.NCOLS = o + self.NSM


def win_perm(C):
    H, G, D = C.H, C.G, C.D
    sizes = (H * 128, H * 128, H * 128, H * 128, H, H, C.NH * 128, G * 128, G * 128, G * 128,
             G * 128, G * 128, G * 128, 3 * C.NH, D, D)
    names = ("gq", "gk", "gv", "gz", "pb", "pa", "nq", "kc", "vc", "ksl", "vsl", "kw", "vw", "ng", "ma", "mb")
    st = {}
    o = 0
    for n, s in zip(names, sizes):
        st[n] = (o, s)
        o += s
    order = ["gq", "gk", "gv", "gz", "ma", "mb", "nq", "kc", "vc", "ksl", "vsl", "kw", "vw", "pb", "pa", "ng"]
    return np.concatenate([np.arange(st[n][0], st[n][0] + st[n][1]) for n in order])


def stage_proj(p, C, x_tok, w, w_s, mods, projF, projB, small, cst):
    D, S, KC = C.D, C.S, C.KC
    TT = min(1024, S)
    CT = 256
    p.push()
    uT = p.sb("uT", [128, KC * TT], BF16)
    wf = [p.sb(f"wf{i}", [128, max(KC * CT, D)]) for i in range(2)]
    wb = [p.sb(f"wb{i}", [128, KC * CT], BF16) for i in range(2)]
    ost = [p.sb(f"ost{i}", [128, 512]) for i in range(4)]
    ostb = [p.sb(f"ostb{i}", [128, 512], BF16) for i in range(4)]
    pss = [p.ps(f"pp{i}", [128, 512]) for i in range(8)]
    sc, sh = mods
    ident = cst["ident"]
    nps = 0
    nw = 0
    no = 0
    for tt0 in range(0, S, TT):
        for kc in range(KC):
            xb = wf[nw % 2]
            nw += 1
            p.dma(xb[:, 0:TT], x_tok[kc * 128:(kc + 1) * 128, tt0:tt0 + TT], writes=[xb])
            p.ts((uT, uT[:, kc * TT:(kc + 1) * TT]), (xb, xb[:, 0:TT]),
                 (sc, sc[:, kc:kc + 1]), (sh, sh[:, kc:kc + 1]), ALU.mult, ALU.add)
        for c0 in range(0, C.NBIG, CT):
            cw = min(CT, C.NBIG - c0)
            wfb = wf[nw % 2]
            wbb = wb[nw % 2]
            nw += 1
            wfv = wfb[:, 0:KC * cw].rearrange("p (k n) -> p k n", n=cw)
            assert cw == CT
            p.dma(wfv, w[c0 // CT], writes=[wfb])
            half = (KC * cw) // 2
            p.cp((wbb, wbb[:, 0:half]), (wfb, wfb[:, 0:half]), eng="dve")
            p.cp((wbb, wbb[:, half:KC * cw]), (wfb, wfb[:, half:KC * cw]), eng="pool")
            for cb in range(0, cw, 128):
                col = c0 + cb
                for ts0 in range(0, TT, 512):
                    tw = min(512, TT - ts0)
                    ps = pss[nps % 8]
                    nps += 1
                    for kc in range(KC):
                        p.mm((ps, ps[:, 0:tw]), (wbb, wbb[:, kc * cw + cb: kc * cw + cb + 128]),
                             (uT, uT[:, kc * TT + ts0: kc * TT + ts0 + tw]), start=(kc == 0), stop=(kc == KC - 1))
                    no += 1
                    if col < C.NF:
                        o = ost[no % 4]
                        p.cp((o, o[:, 0:tw]), (ps, ps[:, 0:tw]), eng="act")
                        p.dma(projF[col:col + 128, tt0 + ts0: tt0 + ts0 + tw], o[:, 0:tw], reads=[o], q="pool")
                    else:
                        o = ostb[no % 4]
                        scl = 128.0 ** -0.5 if col < C.off["kc"] else 1.0
                        p.act((o, o[:, 0:tw]), (ps, ps[:, 0:tw]), AF.Copy, scale=scl)
                        p.dma(projB[col - C.NF: col - C.NF + 128, tt0 + ts0: tt0 + ts0 + tw], o[:, 0:tw],
                              reads=[o], q="pool")
        nsm = C.NSM
        wfb = wf[nw % 2]
        wbb = wb[nw % 2]
        nw += 1
        wfv = wfb[:, 0:KC * nsm].rearrange("p (k n) -> p k n", n=nsm)
        p.dma(wfv, w_s, writes=[wfb])
        p.cp((wbb, wbb[:, 0:KC * nsm]), (wfb, wfb[:, 0:KC * nsm]), eng="dve")
        for tb in range(TT // 128):
            ps = pss[nps % 8]
            nps += 1
            for kc in range(KC):
                p.mm((ps, ps[:, 0:nsm]), (uT, uT[:, kc * TT + tb * 128: kc * TT + (tb + 1) * 128]),
                     (wbb, wbb[:, kc * nsm:(kc + 1) * nsm]), start=(kc == 0), stop=(kc == KC - 1))
            no += 1
            o = ost[no % 4]
            p.cp((o, o[:, 0:nsm]), (ps, ps[:, 0:nsm]), eng="act")
            p.dma(small[tt0 + tb * 128: tt0 + (tb + 1) * 128, :], o[:, 0:nsm], reads=[o], q="pool")
    p.pop()


def load_consts(p, nc, C, tag=""):
    cst = {}
    d = nc.dram_tensor("cst_f32", [128, 6 * 128], F32, kind="ExternalInput").ap()
    t = p.sb("cstf", [128, 6 * 128])
    p.dma(t[:, :], d, writes=[t])
    names = ["ident", "ones", "tri", "maskl", "masku", "maskdiag"]
    for i, n in enumerate(names):
        cst[n] = View(t, t[:, i * 128:(i + 1) * 128])
    tb = p.sb("cstb", [128, 128], BF16)
    p.cp((tb, tb[:, :]), (t, t[:, 0:128]))
    cst["identb"] = View(tb, tb[:, :])
    return cst


class View:
    def __init__(self, buf, ap):
        self.buf = buf
        self.ap = ap

    def __getitem__(self, idx):
        return self.ap[idx]


def consts_np():
    i = np.arange(128)
    ident = np.eye(128, dtype=np.float32)
    ones = np.ones((128, 128), np.float32)
    tri = (i[:, None] <= i[None, :]).astype(np.float32)
    maskl = np.where(i[None, :] < i[:, None], 0.0, -1e30).astype(np.float32)
    masku = np.where(i[:, None] <= i[None, :], 0.0, 1e30).astype(np.float32)
    maskd = (i[:, None] != i[None, :]).astype(np.float32)
    return np.concatenate([ident, ones, tri, maskl, masku, maskd], axis=1)


def bc_mid(view_ap_tile, off, rowlen, n_mid, n_in):
    return bass.AP(view_ap_tile, off, [[rowlen, 128], [0, n_mid], [1, n_in]])


def stage_gdn(p, C, projF, small, gpar, oaT, cst):
    S, H = C.S, C.H
    NCH = S // 128
    NSM = C.NSM
    ident, ones, tri, maskl, masku = cst["ident"], cst["ones"], cst["tri"], cst["maskl"], cst["masku"]
    convw, alog, dtb, normw = gpar
    p.push()
    pss = [p.ps(f"gp{i}", [128, 512]) for i in range(8)]
    nps = [0]

    def PS():
        nps[0] += 1
        return pss[nps[0] % 8]

    SM = p.sb("SM", [128, NCH * NSM])
    p.dma(SM[:, :].rearrange("p (c n) -> p c n", n=NSM), small.rearrange("(c p) n -> p c n", p=128), writes=[SM])
    cw = p.sb("cw", [128, 3 * H * 4]); p.dma(cw[:, :], convw, writes=[cw])
    al = p.sb("al", [128, H]); p.dma(al[:, :], alog, writes=[al])
    db = p.sb("db", [128, H]); p.dma(db[:, :], dtb, writes=[db])
    nw = p.sb("nw", [128, 1]); p.dma(nw[:, :], normw, writes=[nw])
    nea = p.sb("nea", [128, H])
    p.act((nea, nea[:, :]), (al, al[:, :]), AF.Exp)
    p.ts((nea, nea[:, :]), (nea, nea[:, :]), -1.0, None, ALU.mult)
    SM3 = SM[:, :].rearrange("p (c n) -> p c n", n=NSM)
    BETA = p.sb("BETA", [128, NCH * H]); LA = p.sb("LA", [128, NCH * H])
    EG = p.sb("EG", [128, NCH * H]); EGL = p.sb("EGL", [128, NCH * H])
    BG = p.sb("BG", [128, NCH * H]); KE = p.sb("KE", [128, NCH * H])
    v3 = lambda t: t[:, :].rearrange("p (c n) -> p c n", n=H)
    p.act((BETA, v3(BETA)), (SM, SM3[:, :, 0:H]), AF.Sigmoid)
    p.tt((LA, v3(LA)), (SM, SM3[:, :, H:2 * H]), (db, bc_mid(db.t, 0, H, NCH, H)), ALU.add)
    p.act((LA, LA[:, :]), (LA, LA[:, :]), AF.Exp)
    p.act((LA, LA[:, :]), (LA, LA[:, :]), AF.Ln, bias=1.0)
    p.tt((LA, v3(LA)), (LA, v3(LA)), (nea, bc_mid(nea.t, 0, H, NCH, H)), ALU.mult)
    for c in range(NCH):
        sl = slice(c * H, (c + 1) * H)
        pg = PS(); pl = PS()
        p.mm((pg, pg[:, 0:H]), (tri, tri[:, :]), (LA, LA[:, sl]))
        p.mm((pl, pl[:, 0:H]), (ones, ones[:, :]), (LA, LA[:, sl]))
        p.act((EG, EG[:, sl]), (pg, pg[:, 0:H]), AF.Exp)
        p.act((EGL, EGL[:, sl]), (pl, pl[:, 0:H]), AF.Exp)
        p.cp((BG, BG[:, sl]), (pg, pg[:, 0:H]), eng="act")
        p.cp((KE, KE[:, sl]), (pl, pl[:, 0:H]), eng="act")
        p.tt((KE, KE[:, sl]), (KE, KE[:, sl]), (BG, BG[:, sl]), ALU.subtract)
        p.act((KE, KE[:, sl]), (KE, KE[:, sl]), AF.Exp)
        p.tt((BG, BG[:, sl]), (BETA, BETA[:, sl]), (EG, EG[:, sl]), ALU.mult)

    import os as _os
    STOP = int(_os.environ.get("GDN_STOP", "99"))
    if STOP <= 1:
        p.pop(); return
    raw = [p.sb(f"raw{i}", [128, 4 + S]) for i in range(3)]
    cv = [p.sb(f"cv{i}", [128, S]) for i in range(3)]
    zt = p.sb("zt", [128, S])
    rs = p.sb("rs", [128, 512])
    oast = p.sb("oast", [128, S], BF16)
    Sst = p.sb("Sst", [128, 128])
    T = {n: p.sb(n, [128, 128]) for n in
         ("latri", "nlatri", "tmpL", "tmpU", "decL", "decU", "Lm", "qkT", "R", "P0", "P1", "PT1", "PT0",
          "Bk", "kend", "Bv", "WT", "Uu", "ucorr", "o1", "o", "on", "sq")}
    ssq = p.sb("ssq", [128, 2])
    for i in range(3):
        p.memset((raw[i], raw[i][:, 0:4]), 0.0)
    for h in range(H):
        for i, seg in enumerate(("gq", "gk", "gv")):
            r0 = C.off[seg] + h * 128
            p.dma(raw[i][:, 4:4 + S], projF[r0:r0 + 128, :], writes=[raw[i]])
            wcol = lambda j, i=i: (cw, cw[:, (i * H + h) * 4 + j:(i * H + h) * 4 + j + 1])
            e = "dve"
            p.ts((cv[i], cv[i][:, :]), (raw[i], raw[i][:, 1:1 + S]), wcol(0), None, ALU.mult, eng=e)
            for j in (1, 2, 3):
                p.stt((cv[i], cv[i][:, :]), (raw[i], raw[i][:, 1 + j:1 + j + S]), wcol(j), (cv[i], cv[i][:, :]),
                      ALU.mult, ALU.add, eng=e)
            p.act((cv[i], cv[i][:, :]), (cv[i], cv[i][:, :]), AF.Silu)
        r0 = C.off["gz"] + h * 128
        p.dma(zt[:, :], projF[r0:r0 + 128, :], writes=[zt])
        p.act((zt, zt[:, :]), (zt, zt[:, :]), AF.Silu)
        for i in (0, 1):
            sqt = raw[i]
            p.act((sqt, sqt[:, 4:4 + S]), (cv[i], cv[i][:, :]), AF.Square)
            for b0 in range(0, S, 512):
                bw = min(512, S - b0)
                ps = PS()
                p.mm((ps, ps[:, 0:bw]), (ones, ones[:, :]), (sqt, sqt[:, 4 + b0:4 + b0 + bw]))
                p.rsqrt((rs, rs[:, 0:bw]), (ps, ps[:, 0:bw]), 1e-6, 1.0)
                if i == 0:
                    p.stt((cv[i], cv[i][:, b0:b0 + bw]), (cv[i], cv[i][:, b0:b0 + bw]), 128.0 ** -0.5,
                          (rs, rs[:, 0:bw]), ALU.mult, ALU.mult)
                else:
                    p.tt((cv[i], cv[i][:, b0:b0 + bw]), (cv[i], cv[i][:, b0:b0 + bw]), (rs, rs[:, 0:bw]), ALU.mult)
        p.memset((Sst, Sst[:, :]), 0.0)
        for c in range(NCH if STOP > 2 else 0):
            cs = slice(c * 128, (c + 1) * 128)
            qT = (cv[0], cv[0][:, cs]); kT = (cv[1], cv[1][:, cs]); vT = (cv[2], cv[2][:, cs])
            sc_ = lambda t: (t, t[:, c * H + h:c * H + h + 1])
            A = lambda n: (T[n], T[n][:, :])
            p.ts(A("latri"), (tri, tri[:, :]), sc_(LA), None, ALU.mult, eng="pool")
            p.ts(A("nlatri"), (tri, tri[:, :]), sc_(LA), -1.0, ALU.mult, ALU.mult, eng="pool")
            pG = PS()
            p.mm((pG, pG[:, 0:128]), A("latri"), (ones, ones[:, :]), start=True, stop=False)
            p.mm((pG, pG[:, 0:128]), (ones, ones[:, :]), A("nlatri"), start=False, stop=True)
            p.tt(A("tmpL"), (pG, pG[:, 0:128]), (maskl, maskl[:, :]), ALU.add)
            p.act(A("decL"), A("tmpL"), AF.Exp)
            p.tt(A("tmpU"), (pG, pG[:, 0:128]), (masku, masku[:, :]), ALU.add)
            p.act(A("decU"), A("tmpU"), AF.Exp, scale=-1.0)
            pkk = PS(); pkq = PS()
            p.mm((pkk, pkk[:, 0:128]), kT, kT)
            p.mm((pkq, pkq[:, 0:128]), kT, qT)
            p.stt(A("Lm"), (pkk, pkk[:, 0:128]), sc_(BETA), A("decL"), ALU.mult, ALU.mult)
            p.tt(A("qkT"), (pkq, pkq[:, 0:128]), A("decU"), ALU.mult)
            if STOP <= 3:
                continue
            pU = PS()
            p.tr((pU, pU[:, 0:128]), A("Lm"), (ident, ident[:, :]))
            p.tt(A("R"), (ident, ident[:, :]), (pU, pU[:, 0:128]), ALU.subtract)
            p.cp(A("P0"), (pU, pU[:, 0:128]), eng="dve")
            Pc, PTc = "P0", "Lm"
            NIT = int(_os.environ.get("GDN_NIT", "6"))
            for it in range(NIT):
                Pn = "P1" if Pc == "P0" else "P0"
                PTn = "PT1" if PTc in ("Lm", "PT0") else "PT0"
                pa = PS()
                p.mm((pa, pa[:, 0:128]), A(Pc), A(PTc))
                if it < 5:
                    pb_ = PS()
                    p.mm((pb_, pb_[:, 0:128]), A(PTc), A(Pc))
                p.cp(A(PTn), (pa, pa[:, 0:128]), eng="act")
                if it < 5:
                    p.cp(A(Pn), (pb_, pb_[:, 0:128]), eng="dve")
                pr = PS()
                p.mm((pr, pr[:, 0:128]), A(PTn), A("R"))
                p.tt(A("R"), A("R"), (pr, pr[:, 0:128]), ALU.add)
                Pc, PTc = Pn, PTn
            if STOP <= 4:
                continue
            pK = PS(); pV = PS()
            p.tr((pK, pK[:, 0:128]), kT, (ident, ident[:, :]))
            p.tr((pV, pV[:, 0:128]), vT, (ident, ident[:, :]))
            p.ts(A("Bk"), (pK, pK[:, 0:128]), sc_(BG), None, ALU.mult)
            p.ts(A("kend"), (pK, pK[:, 0:128]), sc_(KE), None, ALU.mult)
            p.act(A("Bv"), (pV, pV[:, 0:128]), AF.Copy, scale=sc_(BETA))
            pW = PS(); pUu = PS()
            p.mm((pW, pW[:, 0:128]), A("Bk"), A("R"))
            p.mm((pUu, pUu[:, 0:128]), A("R"), A("Bv"))
            p.cp(A("WT"), (pW, pW[:, 0:128]), eng="act")
            p.cp(A("Uu"), (pUu, pUu[:, 0:128]), eng="dve")
            if STOP <= 5:
                continue
            pws = PS(); po1 = PS()
            p.mm((pws, pws[:, 0:128]), A("WT"), (Sst, Sst[:, :]))
            p.mm((po1, po1[:, 0:128]), qT, (Sst, Sst[:, :]))
            p.tt(A("ucorr"), A("Uu"), (pws, pws[:, 0:128]), ALU.subtract)
            p.act(A("o1"), (po1, po1[:, 0:128]), AF.Copy, scale=sc_(EG))
            po2 = PS(); pS = PS()
            p.mm((po2, po2[:, 0:128]), A("qkT"), A("ucorr"))
            p.mm((pS, pS[:, 0:128]), A("kend"), A("ucorr"))
            p.tt(A("o"), A("o1"), (po2, po2[:, 0:128]), ALU.add)
            p.stt((Sst, Sst[:, :]), (Sst, Sst[:, :]), sc_(EGL), (pS, pS[:, 0:128]), ALU.mult, ALU.add)
            p.act(A("sq"), A("o"), AF.Square, accum=(ssq, ssq[:, 0:1]))
            p.rsqrt((ssq, ssq[:, 1:2]), (ssq, ssq[:, 0:1]), 1e-6, 1.0 / 128)
            p.ts(A("on"), A("o"), (ssq, ssq[:, 1:2]), None, ALU.mult)
            pT_ = PS()
            p.tr((pT_, pT_[:, 0:128]), A("on"), (ident, ident[:, :]))
            p.stt((oast, oast[:, cs]), (pT_, pT_[:, 0:128]), (nw, nw[:, 0:1]), (zt, zt[:, cs]), ALU.mult, ALU.mult)
        p.dma(oaT[h * 128:(h + 1) * 128, :], oast[:, :], reads=[oast], q="pool")
    p.pop()


def nsa_consts_np(C):
    S = C.S
    NQ = S // 128
    NS = S // 64
    ncmp = S // 16 - 1
    t = np.arange(S)
    cmp_end = np.arange(256 * ((ncmp + 255) // 256)) * 16 + 31
    jj = np.arange(len(cmp_end))
    dist = (t[:, None] - cmp_end[None, :]).astype(np.float32)
    valid = (dist >= 0) & (jj[None, :] < ncmp)
    cmpd = np.where(valid, dist, 0.0).astype(np.float32).reshape(NQ, 128, -1)
    cmpm = np.where(valid, 0.0, -1e30).astype(np.float32).reshape(NQ, 128, -1)
    cmp_start = jj * 16
    slc_start = np.arange(NS) * 64
    c2s = ((cmp_end[:, None] >= slc_start[None, :]) & (cmp_start[:, None] <= slc_start[None, :] + 63)
           & (jj[:, None] < ncmp)).astype(np.float32)
    cur = (t // 64)[:, None]
    blk = np.arange(NS)[None, :]
    allowed = blk * 64 <= t[:, None]
    fb = np.where(allowed, 0.0, -1e30)
    fb = np.where(blk == cur - 1, 1e6, fb)
    fb = np.where(blk == cur, 2e6, fb)
    fb = np.where(blk == 0, 3e6, fb)
    forced = (blk == 0) | (blk == cur) | (blk == cur - 1)
    allow = (allowed & ~forced).astype(np.float32)
    tl = np.arange(128)[:, None]
    c = np.arange(S + 128)[None, :]
    relc = (tl + S - c).astype(np.float32)
    cb = np.arange(640)[None, :]
    dband = tl + 512 - cb
    band = np.where((dband >= 0) & (dband < 512), 0.0, -1e30).astype(np.float32)
    kl = np.arange(128)[None, :]
    caus = np.where(kl <= tl, 0.0, -1e30).astype(np.float32)
    return dict(cmpd=cmpd, cmpm=cmpm, c2s=c2s.astype(np.float32), allow=allow.reshape(NQ, 128, NS),
                fb=fb.astype(np.float32).reshape(NQ, 128, NS), relc=relc, band=band, caus=caus)


def stage_nsa(p, C, nc, projB, small, npar, ncst, obT, cst):
    S, G, NH, H = C.S, C.G, C.NH, C.H
    NQ = S // 128
    NS = S // 64
    NCMP = S // 16 - 1
    NJ = ncst["cmpd"].shape[-1]
    NJT = NJ // 128
    slopes = [2.0 ** (-8.0 * (h + 1) / NH) for h in range(NH)]
    pos_d, w1_d, w2_d = npar
    ident, identb = cst["ident"], cst["identb"]
    rowB = lambda seg, g: C.off[seg] - C.NF + g * 128
    p.push()
    pss = [p.ps(f"np{i}", [128, 512]) for i in range(8)]
    nps = [0]

    def PS():
        nps[0] += 1
        return pss[3 + nps[0] % 5]

    npo = [0]

    def PO():
        npo[0] += 1
        return pss[1 + npo[0] % 2]

    relc = p.sb("relc", [128, S + 128]); p.dma(relc[:, :], ncst["relc"], writes=[relc])
    band = p.sb("band", [128, 640]); p.dma(band[:, :], ncst["band"], writes=[band])
    caus = p.sb("caus", [128, 128]); p.dma(caus[:, :], ncst["caus"], writes=[caus])
    c2sf = p.sb("c2sf", [128, NJT * NS]); c2s = p.sb("c2s", [128, NJT * NS], BF16)
    p.dma(c2sf[:, :].rearrange("p (j n) -> p j n", n=NS), ncst["c2s"].rearrange("(j p) n -> p j n", p=128), writes=[c2sf])
    p.cp((c2s, c2s[:, :]), (c2sf, c2sf[:, :]))
    qT = p.sb("qT", [128, 4 * S], BF16)
    KT = {n: p.sb("KT" + n, [128, S], BF16) for n in ("kc", "vc", "ksl", "kw")}
    VT = p.sb("VTt", [128, S], BF16)
    Vtok = {n: p.sb("Vtok" + n, [128, S], BF16) for n in ("vsl", "vw")}
    w1b = p.sb("w1b", [128, 32 * 128], BF16)
    w2f = p.sb("w2f", [128, 128]); w2b = p.sb("w2b", [128, 128], BF16)
    posf = p.sb("posf", [128, 32]); posb = p.sb("posb", [128, 32], BF16)
    cvec = p.sb("cvec", [128, 1])
    hidT = p.sb("hidT", [128, NJ], BF16)
    kcmpT = p.sb("kcmpT", [128, NJ], BF16)
    vcmp = p.sb("vcmp", [128, NJT * 128], BF16)
    sc = p.sb("sc", [128, max(S, 32 * 128)]); pb_ = p.sb("pbf", [128, S], BF16)
    w1f = sc
    selb = p.sb("selb", [128, S])
    pT4 = [p.sb(f"pT4{i}", [128, 512], BF16) for i in range(2)]
    cd = p.sb("cd", [128, NJ]); cm = p.sb("cm", [128, NJ])
    alw = p.sb("alw", [128, NS]); fbt = p.sb("fbt", [128, NS])
    score = p.sb("score", [128, NS]); sc2 = p.sb("sc2", [128, NS]); m8 = p.sb("m8", [128, 16])
    selm = p.sb("selm", [128, NS])
    st = p.sb("st", [128, 8])
    gat = p.sb("gat", [128, 3 * NH])
    obh = [p.sb(f"obh{i}", [128, 128]) for i in range(4)]
    obst = p.sb("obst", [128, 4 * S], BF16)
    npt = [0]

    def softmax_pv(nk, Vt, voff, po):
        p.op("dve", lambda e: e.reduce_max(out=st[:, 0:1], in_=sc[:, 0:nk], axis=AX.X), reads=[sc], writes=[st])
        p.ts((st, st[:, 1:2]), (st, st[:, 0:1]), -1e20, -1.0, ALU.max, ALU.mult)
        p.act((pb_, pb_[:, 0:nk]), (sc, sc[:, 0:nk]), AF.Exp, bias=(st, st[:, 1:2]), accum=(st, st[:, 2:3]))
        p.ts((st, st[:, 3:4]), (st, st[:, 2:3]), 1e-30, None, ALU.add)
        p.op("dve", lambda e: e.reciprocal(out=st[:, 4:5], in_=st[:, 3:4]), reads=[st], writes=[st])
        p.ts((pb_, pb_[:, 0:nk]), (pb_, pb_[:, 0:nk]), (st, st[:, 4:5]), None, ALU.mult, eng="pool")
        nt = (nk + 127) // 128
        for k4 in range(0, nt, 4):
            n4 = min(4, nt - k4)
            pt = PS()
            for i in range(n4):
                kw_ = min(128, nk - (k4 + i) * 128)
                p.mm((pt, pt[0:kw_, i * 128:(i + 1) * 128]), (pb_, pb_[:, (k4 + i) * 128:(k4 + i) * 128 + kw_]),
                     (identb, identb[:, :]))
            t4 = pT4[npt[0] % 2]
            npt[0] += 1
            if nk - k4 * 128 >= n4 * 128:
                p.cp((t4, t4[:, 0:n4 * 128]), (pt, pt[:, 0:n4 * 128]), eng="act")
            else:
                for i in range(n4):
                    kw_ = min(128, nk - (k4 + i) * 128)
                    p.cp((t4, t4[0:kw_, i * 128:(i + 1) * 128]), (pt, pt[0:kw_, i * 128:(i + 1) * 128]), eng="act")
            for i in range(n4):
                kt = k4 + i
                kw_ = min(128, nk - kt * 128)
                p.mm((po, po[:, 0:128]), (t4, t4[0:kw_, i * 128:(i + 1) * 128]),
                     (Vt, Vt[0:kw_, voff + kt * 128: voff + (kt + 1) * 128]), start=(kt == 0), stop=(kt == nt - 1))
            yield k4, n4, t4

    for g in range(G):
        for hh in range(4):
            r0 = rowB("nq", 0) + (g * 4 + hh) * 128
            p.dma(qT[:, hh * S:(hh + 1) * S], projB[r0:r0 + 128, :], writes=[qT])
        for n in ("kc", "vc", "ksl", "kw"):
            r0 = rowB(n, g)
            p.dma(KT[n][:, :], projB[r0:r0 + 128, :], writes=[KT[n]])
        for n in ("vsl", "vw"):
            r0 = rowB(n, g)
            p.dma(VT[:, :], projB[r0:r0 + 128, :], writes=[VT])
            for kt in range(0, NQ, 4):
                n4 = min(4, NQ - kt)
                pt = PS()
                for i in range(n4):
                    p.mm((pt, pt[:, i * 128:(i + 1) * 128]), (VT, VT[:, (kt + i) * 128:(kt + i + 1) * 128]), (identb, identb[:, :]))
                p.cp((Vtok[n], Vtok[n][:, kt * 128:(kt + n4) * 128]), (pt, pt[:, 0:n4 * 128]), eng="act")
        for kv, n in enumerate(("kc", "vc")):
            p.dma(w1f[:, :].rearrange("d (r e) -> d r e", e=128), w1_d[kv].rearrange("r d e -> d r e"), writes=[w1f])
            p.cp((w1b, w1b[:, :]), (w1f, w1f[:, :]))
            p.dma(w2f[:, :], w2_d[kv], writes=[w2f]); p.cp((w2b, w2b[:, :]), (w2f, w2f[:, :]))
            p.dma(posf[:, :], pos_d[kv], writes=[posf]); p.cp((posb, posb[:, :]), (posf, posf[:, :]))
            pc = PS()
            for r in range(32):
                p.mm((pc, pc[:, 0:1]), (w1b, w1b[:, r * 128:(r + 1) * 128]), (posb, posb[:, r:r + 1]), start=(r == 0), stop=(r == 31))
            p.cp((cvec, cvec[:, :]), (pc, pc[:, 0:1]))
            p.memset((hidT, hidT[:, :]), 0.0)
            for j0 in range(0, NCMP, 512):
                jw = min(512, NCMP - j0)
                ph = PS()
                for r in range(32):
                    lo = 16 * j0 + r
                    p.mm((ph, ph[:, 0:jw]), (w1b, w1b[:, r * 128:(r + 1) * 128]),
                         (KT[n], KT[n][:, lo: lo + 16 * (jw - 1) + 1: 16]), start=(r == 0), stop=(r == 31))
                p.act((hidT, hidT[:, j0:j0 + jw]), (ph, ph[:, 0:jw]), AF.Silu, bias=(cvec, cvec[:, 0:1]))
            if kv == 0:
                for j0 in range(0, NJ, 512):
                    jw = min(512, NJ - j0)
                    pk = PS()
                    p.mm((pk, pk[:, 0:jw]), (w2b, w2b[:, :]), (hidT, hidT[:, j0:j0 + jw]))
                    p.cp((kcmpT, kcmpT[:, j0:j0 + jw]), (pk, pk[:, 0:jw]), eng="act")
            else:
                for jt in range(NJT):
                    pv = PS()
                    p.mm((pv, pv[:, 0:128]), (hidT, hidT[:, jt * 128:(jt + 1) * 128]), (w2b, w2b[:, :]))
                    p.cp((vcmp, vcmp[:, jt * 128:(jt + 1) * 128]), (pv, pv[:, 0:128]), eng="act")
        for qi in range(NQ):
            t0 = qi * 128
            p.dma(cd[:, :], ncst["cmpd"][qi], writes=[cd]); p.dma(cm[:, :], ncst["cmpm"][qi], writes=[cm])
            p.dma(alw[:, :], ncst["allow"][qi], writes=[alw]); p.dma(fbt[:, :], ncst["fb"][qi], writes=[fbt])
            p.dma(gat[:, :], small[t0:t0 + 128, 2 * H:2 * H + 3 * NH], writes=[gat])
            p.act((gat, gat[:, :]), (gat, gat[:, :]), AF.Sigmoid)
            njv = min(NJ, ((t0 + 96) // 16 + 1 + 127) // 128 * 128)
            pimp = pss[0]
            for hh in range(4):
                hg = g * 4 + hh
                q_ap = (qT, qT[:, hh * S + t0: hh * S + t0 + 128])
                for j0 in range(0, njv, 512):
                    jw = min(512, njv - j0)
                    pscr = PS()
                    p.mm((pscr, pscr[:, 0:jw]), q_ap, (kcmpT, kcmpT[:, j0:j0 + jw]))
                    p.stt((sc, sc[:, j0:j0 + jw]), (cd, cd[:, j0:j0 + jw]), -slopes[hg], (pscr, pscr[:, 0:jw]), ALU.mult, ALU.add)
                    p.tt((sc, sc[:, j0:j0 + jw]), (sc, sc[:, j0:j0 + jw]), (cm, cm[:, j0:j0 + jw]), ALU.add, eng="pool")
                po = PO()
                for k4, n4, t4 in softmax_pv(njv, vcmp, 0, po):
                    for i in range(n4):
                        jt = k4 + i
                        first = (hh == 0 and jt == 0)
                        last = (hh == 3 and jt == njv // 128 - 1)
                        p.mm((pimp, pimp[:, 0:NS]), (t4, t4[:, i * 128:(i + 1) * 128]), (c2s, c2s[:, jt * NS:(jt + 1) * NS]),
                             start=first, stop=last)
                obt = obh[hh]
                p.ts((obt, obt[:, :]), (po, po[:, 0:128]), (gat, gat[:, 0 * NH + hg:0 * NH + hg + 1]), None, ALU.mult)
            nkeys = t0 + 128
            p.tt((score, score[:, :]), (pimp, pimp[:, 0:NS]), (alw, alw[:, :]), ALU.mult)
            p.tt((score, score[:, :]), (score, score[:, :]), (fbt, fbt[:, :]), ALU.add)
            if NS > 16:
                p.op("dve", lambda e: e.max(out=m8[:, 0:8], in_=score[:, :]), reads=[score], writes=[m8])
                p.op("dve", lambda e: e.match_replace(out=sc2[:, :], in_to_replace=m8[:, 0:8], in_values=score[:, :],
                                                      imm_value=-2e30), reads=[m8, score], writes=[sc2])
                p.op("dve", lambda e: e.max(out=m8[:, 8:16], in_=sc2[:, :]), reads=[sc2], writes=[m8])
                p.ts((m8, m8[:, 0:1]), (m8, m8[:, 15:16]), -1e29, None, ALU.max)
                p.ts((selm, selm[:, :]), (score, score[:, :]), (m8, m8[:, 0:1]), None, ALU.is_ge)
            else:
                p.ts((selm, selm[:, :]), (score, score[:, :]), -1e29, None, ALU.is_ge)
            p.ts((selm, selm[:, :]), (selm, selm[:, :]), -1.0, 1e30, ALU.add, ALU.mult)
            nb = nkeys // 64
            p.op("dve", lambda e, nb=nb: e.tensor_copy(
                out=selb[:, 0:nb * 64].rearrange("p (n k) -> p n k", k=64),
                in_=bass.AP(selm.t, 0, [[NS, 128], [1, nb], [0, 64]])), reads=[selm], writes=[selb])
            p.tt((selb, selb[:, t0:t0 + 128]), (selb, selb[:, t0:t0 + 128]), (caus, caus[:, :]), ALU.add)
            for hh in range(4):
                hg = g * 4 + hh
                q_ap = (qT, qT[:, hh * S + t0: hh * S + t0 + 128])
                obt = obh[hh]
                for br, (kn, vn) in ((1, ("ksl", "vsl")), (2, ("kw", "vw"))):
                    klo = 0 if br == 1 else max(0, t0 - 512)
                    nk = nkeys - klo
                    for kb in range(0, nk, 512):
                        bw = min(512, nk - kb)
                        pscr = PS()
                        p.mm((pscr, pscr[:, 0:bw]), q_ap, (KT[kn], KT[kn][:, klo + kb: klo + kb + bw]))
                        c0 = klo + kb - t0 + S
                        p.stt((sc, sc[:, kb:kb + bw]), (relc, relc[:, c0:c0 + bw]), -slopes[hg], (pscr, pscr[:, 0:bw]), ALU.mult, ALU.add)
                        if br == 1:
                            p.tt((sc, sc[:, kb:kb + bw]), (sc, sc[:, kb:kb + bw]), (selb, selb[:, kb:kb + bw]), ALU.add, eng="pool")
                        else:
                            b0 = klo + kb - (t0 - 512)
                            p.tt((sc, sc[:, kb:kb + bw]), (sc, sc[:, kb:kb + bw]), (band, band[:, b0:b0 + bw]), ALU.add, eng="pool")
                    po = PO()
                    for _ in softmax_pv(nk, Vtok[vn], klo, po):
                        pass
                    p.stt((obt, obt[:, :]), (po, po[:, 0:128]), (gat, gat[:, br * NH + hg: br * NH + hg + 1]), (obt, obt[:, :]), ALU.mult, ALU.add)
                ptr = PS()
                p.tr((ptr, ptr[:, 0:128]), (obt, obt[:, :]), (ident, ident[:, :]))
                p.cp((obst, obst[:, hh * S + t0: hh * S + t0 + 128]), (ptr, ptr[:, 0:128]), eng="act")
        for hh in range(4):
            r0 = (g * 4 + hh) * 128
            p.dma(obT[r0:r0 + 128, :], obst[:, hh * S:(hh + 1) * S], reads=[obst], q="pool")
    p.pop()


class LinCtx:
    def __init__(self, p, nacc=4, CT=256):
        self.p = p
        self.CT = CT
        self.wf = [p.sb(f"lwf{i}", [128, 32 * CT]) for i in range(2)]
        self.wb = [p.sb(f"lwb{i}", [128, 32 * CT], BF16) for i in range(2)]
        self.acc = [p.ps(f"lacc{i}", [128, 512]) for i in range(nacc)]
        self.nw = 0
        self.na = 0


def tile_w(W, CT):
    K, N = W.shape
    assert N % CT == 0 and K % 128 == 0
    return np.ascontiguousarray(W.reshape(K // 128, 128, N // CT, CT).transpose(2, 1, 0, 3))


def linear_fm(L, Wt, nk, act_fn, tokw, consume, ct0=0, nct=None, k_off=0, kgrp=32):
    p = L.p
    CT = L.CT
    assert Wt.shape[3] == CT
    if nct is None:
        nct = Wt.shape[0] - ct0
    for ct in range(ct0, ct0 + nct):
        c0 = ct * CT
        cw = CT
        nblk = (cw + 127) // 128
        accs = []
        for b in range(nblk):
            accs.append(L.acc[L.na % len(L.acc)])
            L.na += 1
        for k0 in range(0, nk, kgrp):
            kg = min(kgrp, nk - k0)
            wfb = L.wf[L.nw % 2]
            wbb = L.wb[L.nw % 2]
            L.nw += 1
            p.dma(wfb[:, 0:kg * cw].rearrange("p (k n) -> p k n", n=cw), Wt[ct, :, k_off + k0:k_off + k0 + kg, :], writes=[wfb])
            half = (kg * cw) // 2
            p.cp((wbb, wbb[:, 0:half]), (wfb, wfb[:, 0:half]), eng="dve")
            p.cp((wbb, wbb[:, half:kg * cw]), (wfb, wfb[:, half:kg * cw]), eng="pool")
            for b in range(nblk):
                bw = min(128, cw - b * 128)
                for k in range(kg):
                    p.mm((accs[b], accs[b][0:bw, 0:tokw]), (wbb, wbb[:, k * cw + b * 128: k * cw + b * 128 + bw]),
                         act_fn(k0 + k), start=(k0 + k == 0), stop=(k0 + k == nk - 1))
        for b in range(nblk):
            consume(c0 + b * 128, accs[b])


def layernorm_fm(p, C, rT, TT, lng, lnb, outT_d, tt0, cst, tmp, stat_ps, mean_t, rstd_t, ost):
    KC, D = C.KC, C.D
    ones = cst["ones"]
    s1, s2 = stat_ps
    for kc in range(KC):
        p.mm((s1, s1[:, 0:TT]), (ones, ones[:, :]), (rT, rT[:, kc * TT:(kc + 1) * TT]), start=(kc == 0), stop=(kc == KC - 1))
    for kc in range(KC):
        t = tmp[kc % 2]
        p.act((t, t[:, 0:TT]), (rT, rT[:, kc * TT:(kc + 1) * TT]), AF.Square)
        p.mm((s2, s2[:, 0:TT]), (ones, ones[:, :]), (t, t[:, 0:TT]), start=(kc == 0), stop=(kc == KC - 1))
    p.act((mean_t, mean_t[:, 0:TT]), (s1, s1[:, 0:TT]), AF.Copy, scale=1.0 / D)
    t = tmp[0]
    p.tt((t, t[:, 0:TT]), (mean_t, mean_t[:, 0:TT]), (mean_t, mean_t[:, 0:TT]), ALU.mult)
    p.stt((rstd_t, rstd_t[:, 0:TT]), (s2, s2[:, 0:TT]), 1.0 / D, (t, t[:, 0:TT]), ALU.mult, ALU.subtract)
    p.rsqrt((rstd_t, rstd_t[:, 0:TT]), (rstd_t, rstd_t[:, 0:TT]), 1e-5, 1.0)
    for kc in range(KC):
        o = ost[kc % 2]
        e = "dve" if kc % 2 == 0 else "pool"
        p.tt((o, o[:, 0:TT]), (rT, rT[:, kc * TT:(kc + 1) * TT]), (mean_t, mean_t[:, 0:TT]), ALU.subtract, eng=e)
        p.tt((o, o[:, 0:TT]), (o, o[:, 0:TT]), (rstd_t, rstd_t[:, 0:TT]), ALU.mult, eng=e)
        p.ts((o, o[:, 0:TT]), (o, o[:, 0:TT]), (lng, lng[:, kc:kc + 1]), (lnb, lnb[:, kc:kc + 1]), ALU.mult, ALU.add, eng=e)
        p.dma(outT_d[kc * 128:(kc + 1) * 128, tt0:tt0 + TT], o[:, 0:TT], reads=[o], q="pool")


def stage_post1(p, C, xT_d, oaT, obT, projF, wra, wrb, wout, g1p, lng, lnb, x1T_d, cst, alpha):
    S, KC, H, NH, D = C.S, C.KC, C.H, C.NH, C.D
    TT = min(512, S)
    p.push()
    L = LinCtx(p, CT=128)
    oa = p.sb("oa", [128, H * TT], BF16); obt = p.sb("obt", [128, NH * TT], BF16)
    yT = p.sb("yT", [128, KC * TT], BF16)
    rT = p.sb("rT", [128, KC * TT])
    sg = [p.sb(f"sg{i}", [128, TT]) for i in range(2)]
    ya = p.sb("ya", [128, TT])
    xb = [p.sb(f"xb{i}", [128, TT]) for i in range(2)]
    tmp = [p.sb(f"ltmp{i}", [128, TT]) for i in range(2)]
    ost = [p.sb(f"lost{i}", [128, TT]) for i in range(2)]
    mean_t = p.sb("mean_t", [128, TT]); rstd_t = p.sb("rstd_t", [128, TT])
    stat_ps = (p.ps("s1", [128, 512]), p.ps("s2", [128, 512]))
    for tt0 in range(0, S, TT):
        p.dma(oa[:, :].rearrange("p (k t) -> p k t", t=TT), oaT.rearrange("(k p) t -> p k t", p=128)[:, :, tt0:tt0 + TT], writes=[oa])
        p.dma(obt[:, :].rearrange("p (k t) -> p k t", t=TT), obT.rearrange("(k p) t -> p k t", p=128)[:, :, tt0:tt0 + TT], writes=[obt])

        def cons_a(c, ps):
            s = sg[0]
            r0 = C.off["ma"] + c
            p.dma(s[:, 0:TT], projF[r0:r0 + 128, tt0:tt0 + TT], writes=[s], q="act")
            p.act((s, s[:, 0:TT]), (s, s[:, 0:TT]), AF.Sigmoid)
            kc = c // 128
            p.tt((yT, yT[:, kc * TT:(kc + 1) * TT]), (ps, ps[:, 0:TT]), (s, s[:, 0:TT]), ALU.mult)

        def cons_b(c, ps):
            s = sg[1]
            r0 = C.off["mb"] + c
            p.dma(s[:, 0:TT], projF[r0:r0 + 128, tt0:tt0 + TT], writes=[s], q="act")
            p.act((s, s[:, 0:TT]), (s, s[:, 0:TT]), AF.Sigmoid)
            kc = c // 128
            p.tt((ya, ya[:, 0:TT]), (ps, ps[:, 0:TT]), (s, s[:, 0:TT]), ALU.mult)
            p.tt((yT, yT[:, kc * TT:(kc + 1) * TT]), (yT, yT[:, kc * TT:(kc + 1) * TT]), (ya, ya[:, 0:TT]), ALU.add, eng="pool")

        linear_fm(L, wra, H, lambda k: (oa, oa[:, k * TT:(k + 1) * TT]), TT, cons_a)
        linear_fm(L, wrb, NH, lambda k: (obt, obt[:, k * TT:(k + 1) * TT]), TT, cons_b)

        def cons_o(c, ps):
            kc = c // 128
            x = xb[kc % 2]
            p.dma(x[:, 0:TT], xT_d[c:c + 128, tt0:tt0 + TT], writes=[x], q="act")
            t = tmp[kc % 2]
            p.act((t, t[:, 0:TT]), (ps, ps[:, 0:TT]), AF.Copy, scale=(g1p, g1p[:, kc:kc + 1]))
            p.stt((rT, rT[:, kc * TT:(kc + 1) * TT]), (x, x[:, 0:TT]), alpha, (t, t[:, 0:TT]), ALU.mult, ALU.add)

        linear_fm(L, wout, KC, lambda k: (yT, yT[:, k * TT:(k + 1) * TT]), TT, cons_o)
        layernorm_fm(p, C, rT, TT, lng, lnb, x1T_d, tt0, cst, tmp, stat_ps, mean_t, rstd_t, ost)
    p.pop()


def stage_post2(p, C, x1T_d, w1, b1, w2, b2, sc2p, sh2, g2p, lng, lnb, x2T_d, cst, alpha):
    S, KC, D, DFF = C.S, C.KC, C.D, C.DFF
    FC = DFF // 128
    FG = min(32, FC)
    TT = min(512, S)
    p.push()
    L = LinCtx(p, CT=128)
    uT = p.sb("uT2", [128, KC * TT], BF16)
    xk = p.sb("xk", [128, KC * TT])
    h1T = p.sb("h1T", [128, FG * TT], BF16)
    gb = p.sb("gb", [128, KC])
    tmp = [p.sb(f"mtmp{i}", [128, TT]) for i in range(2)]
    ost = [p.sb(f"most{i}", [128, TT]) for i in range(2)]
    mean_t = p.sb("mean_t2", [128, TT]); rstd_t = p.sb("rstd_t2", [128, TT])
    stat_ps = (p.ps("s1b", [128, 512]), p.ps("s2b", [128, 512]))
    p.tt((gb, gb[:, :]), (g2p, g2p[:, :]), (b2, b2[:, :]), ALU.mult)
    for tt0 in range(0, S, TT):
        p.dma(xk[:, :].rearrange("p (k t) -> p k t", t=TT), x1T_d.rearrange("(k p) t -> p k t", p=128)[:, :, tt0:tt0 + TT], writes=[xk])
        for kc in range(KC):
            sl = slice(kc * TT, (kc + 1) * TT)
            p.ts((uT, uT[:, sl]), (xk, xk[:, sl]), (sc2p, sc2p[:, kc:kc + 1]), (sh2, sh2[:, kc:kc + 1]), ALU.mult, ALU.add)
            p.ts((xk, xk[:, sl]), (xk, xk[:, sl]), alpha, (gb, gb[:, kc:kc + 1]), ALU.mult, ALU.add)
        for f0 in range(0, FC, FG):
            def cons_h(c, ps, f0=f0):
                fc = c // 128
                t = tmp[fc % 2]
                p.act((t, t[:, 0:TT]), (ps, ps[:, 0:TT]), AF.Relu, bias=(b1, b1[:, fc:fc + 1]))
                fl = fc - f0
                p.tt((h1T, h1T[:, fl * TT:(fl + 1) * TT]), (t, t[:, 0:TT]), (t, t[:, 0:TT]), ALU.mult, eng=("dve" if fc % 2 else "pool"))

            linear_fm(L, w1, KC, lambda k: (uT, uT[:, k * TT:(k + 1) * TT]), TT, cons_h, ct0=f0, nct=FG)

            def cons_o(c, ps):
                kc = c // 128
                sl = slice(kc * TT, (kc + 1) * TT)
                p.stt((xk, xk[:, sl]), (ps, ps[:, 0:TT]), (g2p, g2p[:, kc:kc + 1]), (xk, xk[:, sl]), ALU.mult, ALU.add)

            linear_fm(L, w2, FG, lambda k: (h1T, h1T[:, k * TT:(k + 1) * TT]), TT, cons_o, k_off=f0)
        layernorm_fm(p, C, xk, TT, lng, lnb, x2T_d, tt0, cst, tmp, stat_ps, mean_t, rstd_t, ost)
    p.pop()


PV_NAMES = ["sc1", "sh1", "g1", "sc2", "sh2", "g2"]


def build_seq(C, ncn_shapes):
    nc = bass.Bass("TRN2", target_bir_lowering=False)
    L, D, S, KC, H, NH = C.L, C.D, C.S, C.KC, C.H, C.NH
    di = lambda n, sh, dt=F32: nc.dram_tensor(n, list(sh), dt, kind="ExternalInput").ap()
    xT0 = di("xT", [D, S])
    cT = di("cT", [128, KC])
    ada_w = di("ada_w", [6 * D // 256, 128, KC, 256])
    adabT = di("adabT", [128, 6 * KC])
    tabT = di("tabT", [L, 128, 6 * KC])
    w_in = di("w_in", [L, C.NBIG // 256, 128, KC, 256])
    w_sm = di("w_sm", [L, 128, KC, C.NSM])
    convw = di("convw", [L, 128, 3 * H * 4]); alog = di("alog", [L, 128, H]); dtb = di("dtb", [L, 128, H])
    normw = di("normw", [L, 128, 1])
    cpos = di("cpos", [L, 2, 128, 32]); cw1 = di("cw1", [L, 2, 32, 128, 128]); cw2 = di("cw2", [L, 2, 128, 128])
    wra = di("wra", [L, KC, 128, H, 128]); wrb = di("wrb", [L, KC, 128, NH, 128]); wout = di("wout", [L, KC, 128, KC, 128])
    w1 = di("w1", [L, C.DFF // 128, 128, KC, 128]); w2 = di("w2", [L, KC, 128, C.DFF // 128, 128])
    vecs = di("vecs", [L, 128, 5 * KC + C.DFF // 128])
    ncst = {k: di("nc_" + k, v) for k, v in ncn_shapes.items()}
    outT = nc.dram_tensor("outT", [D, S], F32, kind="ExternalOutput").ap()
    ds = lambda n, sh, dt=F32: nc.dram_tensor(n, list(sh), dt, kind="Internal").ap()
    projF = ds("projF", [C.NF, S]); projB = ds("projB", [C.NB, S], BF16); small = ds("small", [S, C.NSM])
    oaT = ds("oaT", [H * 128, S], BF16); obT = ds("obT", [NH * 128, S], BF16)
    x1T = ds("x1T", [D, S]); xmid = ds("xmid", [D, S])
    alpha = (2.0 * L) ** 0.25
    p = Prog(nc)
    cst = load_consts(p, nc, C)
    md = p.sb("md", [128, 6 * KC])
    p.push()
    L0 = LinCtx(p)
    csf = p.sb("csf", [128, KC]); p.dma(csf[:, :], cT, writes=[csf])
    csb = p.sb("csb", [128, KC], BF16)
    p.act((csb, csb[:, :]), (csf, csf[:, :]), AF.Silu)
    adab = p.sb("adab", [128, 6 * KC]); p.dma(adab[:, :], adabT, writes=[adab])

    def cons_mod(c, ps):
        j = c // 128
        p.tt((md, md[:, j:j + 1]), (ps, ps[:, 0:1]), (adab, adab[:, j:j + 1]), ALU.add)

    linear_fm(L0, ada_w, KC, lambda k: (csb, csb[:, k:k + 1]), 1, cons_mod)
    p.pop()
    xin = xT0
    for l in range(L):
        p.push()
        mt = p.sb("mt", [128, 6 * KC]); p.dma(mt[:, :], tabT[l], writes=[mt])
        p.tt((mt, mt[:, :]), (mt, mt[:, :]), (md, md[:, :]), ALU.add)
        for j in (1, 2, 4, 5):
            p.ts((mt, mt[:, j * KC:(j + 1) * KC]), (mt, mt[:, j * KC:(j + 1) * KC]), 1.0, None, ALU.add)
        V = lambda j: View(mt, mt[:, j * KC:(j + 1) * KC])
        sh1, sc1p, g1p, sh2, sc2p, g2p = (V(j) for j in range(6))
        vt = p.sb("vt", [128, 5 * KC + C.DFF // 128]); p.dma(vt[:, :], vecs[l], writes=[vt])
        W = lambda j: View(vt, vt[:, j * KC:(j + 1) * KC])
        ln1g, ln1b, b2v, ln2g, ln2b = (W(j) for j in range(5))
        b1v = View(vt, vt[:, 5 * KC:5 * KC + C.DFF // 128])
        xout = outT if l == L - 1 else xmid
        stage_proj(p, C, xin, w_in[l], w_sm[l], (sc1p, sh1), projF, projB, small, cst)
        stage_gdn(p, C, projF, small, (convw[l], alog[l], dtb[l], normw[l]), oaT, cst)
        stage_nsa(p, C, nc, projB, small, (cpos[l], cw1[l], cw2[l]), ncst, obT, cst)
        stage_post1(p, C, xin, oaT, obT, projF, wra[l], wrb[l], wout[l], g1p, ln1g, ln1b, x1T, cst, alpha)
        stage_post2(p, C, x1T, w1[l], b1v, w2[l], b2v, sc2p, sh2, g2p, ln2g, ln2b, xout, cst, alpha)
        p.pop()
        xin = xmid
    p.finish()
    return nc, p


def fm(v, kc):
    return np.ascontiguousarray(v.reshape(kc, 128).T)


def seq_inputs(C, b, x, inp):
    L, KC, H = C.L, C.KC, C.H
    perm = win_perm(C)
    m = {}
    m["xT"] = np.ascontiguousarray(x[b].T)
    m["cT"] = fm(inp["c"][b], KC)
    m["ada_w"] = tile_w(inp["ada_w"], 256)
    m["adabT"] = fm(inp["ada_b"], 6 * KC)
    m["tabT"] = np.stack([np.concatenate([fm(inp["ada_table"][l, j], KC) for j in range(6)], axis=1) for l in range(L)])
    wp = [inp["w_in"][l][:, perm] for l in range(L)]
    m["w_in"] = np.stack([tile_w(w[:, :C.NBIG], 256) for w in wp])
    m["w_sm"] = np.stack([np.ascontiguousarray(w[:, C.NBIG:].reshape(KC, 128, C.NSM).transpose(1, 0, 2)) for w in wp])
    m["convw"] = np.stack([inp["gdn_conv_w"][l].reshape(4, 3 * H, 128).transpose(2, 1, 0).reshape(128, 3 * H * 4) for l in range(L)])
    m["alog"] = np.ascontiguousarray(np.broadcast_to(inp["gdn_a_log"][:, None, :], (L, 128, H)))
    m["dtb"] = np.ascontiguousarray(np.broadcast_to(inp["gdn_dt_bias"][:, None, :], (L, 128, H)))
    m["normw"] = np.ascontiguousarray(inp["gdn_norm_w"][:, :, None])
    m["cpos"] = np.ascontiguousarray(inp["cmp_pos"].transpose(0, 1, 3, 2))
    m["cw1"] = inp["cmp_w1"]; m["cw2"] = inp["cmp_w2"]
    for k, src in (("wra", "w_read_a"), ("wrb", "w_read_b"), ("wout", "w_out"), ("w1", "mlp_w1"), ("w2", "mlp_w2")):
        m[k] = np.stack([tile_w(inp[src][l], 128) for l in range(L)])
    m["vecs"] = np.stack([np.concatenate([fm(inp[k][l], KC) for k in ("ln1_g", "ln1_b", "mlp_b2", "ln2_g", "ln2_b")]
                                         + [fm(inp["mlp_b1"][l], C.DFF // 128)], axis=1) for l in range(L)])
    m["cst_f32"] = consts_np()
    for k, v in nsa_consts_np(C).items():
        m["nc_" + k] = v
    return {k: np.ascontiguousarray(v, dtype=np.float32) for k, v in m.items()}


def kernel(**inp):
    inp = {k: np.asarray(v) for k, v in inp.items()}
    x = inp["x"]
    B = x.shape[0]
    C = Cfg()
    ncn = nsa_consts_np(C)
    nc, _ = build_seq(C, {k: v.shape for k, v in ncn.items()})
    base = seq_inputs(C, 0, x, inp)
    in_maps = [base]
    for b in range(1, B):
        m = dict(base)
        m["xT"] = np.ascontiguousarray(x[b].T, dtype=np.float32)
        m["cT"] = np.ascontiguousarray(fm(inp["c"][b], C.KC), dtype=np.float32)
        in_maps.append(m)
    res = run_bass_kernel_spmd(nc, in_maps, core_ids=list(range(B))).results
    return np.stack([np.ascontiguousarray(r["outT"].T) for r in res]).astype(np.float32)
```
